# Optimizing a Trainium2 kernel written in Bass

```python
import math
import jax, jax.numpy as jnp
from jax import lax
import numpy as np

D_MODEL = 2048
BATCH = 4
SEQ = 2048
DEPTH = 4
DEC_BATCH = 32
DEC_SEQ = 4
PAST_LEN = 16384
PAGE_SIZE = 128

HEAD_DIM = 64
N_HEADS = 16
N_KV = 4
GROUP = N_HEADS // N_KV
D_Q = N_HEADS * HEAD_DIM
D_KV = N_KV * HEAD_DIM
WINDOW = 128
BLOCK = 128
ROPE_THETA = 10000.0
D_CONV = D_MODEL // 2
CONV_W = 3
D_FF = 5632
N_IN = D_Q + 2 * D_KV + 3 * D_CONV + 2 * D_MODEL
ALPHA = (2.0 * DEPTH) ** 0.25
BETA = (8.0 * DEPTH) ** -0.25
LN_EPS = 1e-5

kernel_name = "hybrid_swa_sink_shortconv_macaron_deepnorm_step"


def layer_norm(x, g, b):
    xf = x.astype(jnp.float32)
    mu = jnp.mean(xf, axis=-1, keepdims=True)
    var = jnp.mean(jnp.square(xf - mu), axis=-1, keepdims=True)
    y = (xf - mu) * lax.rsqrt(var + LN_EPS) * g.astype(jnp.float32) + b.astype(jnp.float32)
    return y.astype(x.dtype)


def swiglu(x, w_gu, w_down):
    g, u = jnp.split(x @ w_gu, 2, axis=-1)
    return (jax.nn.silu(g) * u) @ w_down


def rope(x, pos):
    inv_freq = ROPE_THETA ** (-jnp.arange(0, HEAD_DIM, 2, dtype=jnp.float32) / HEAD_DIM)
    ang = pos.astype(jnp.float32)[:, None] * inv_freq[None, :]
    cos = jnp.cos(ang)[None, :, None, :].astype(x.dtype)
    sin = jnp.sin(ang)[None, :, None, :].astype(x.dtype)
    x1, x2 = jnp.split(x, 2, axis=-1)
    return jnp.concatenate([x1 * cos - x2 * sin, x2 * cos + x1 * sin], axis=-1)


def sink_window_attention(q, k, v, q_pos, k_pos, sinks):
    s = jnp.einsum('bnqkgd,bnskd->bnkgqs', q, k).astype(jnp.float32) * (HEAD_DIM ** -0.5)
    diff = q_pos[:, :, None] - k_pos[:, None, :]
    mask = (diff >= 0) & (diff <= WINDOW) & (k_pos[:, None, :] >= 0)
    s = jnp.where(mask[None, :, None, None, :, :], s, -jnp.inf)
    sink = sinks.astype(jnp.float32).reshape(N_KV, GROUP)[None, None, :, :, None, None]
    sink = jnp.broadcast_to(sink, s.shape[:-1] + (1,))
    p = jax.nn.softmax(jnp.concatenate([s, sink], axis=-1), axis=-1)[..., :-1]
    return jnp.einsum('bnkgqs,bnskd->bnqkgd', p.astype(v.dtype), v)


def token_mix(x, past, w_in, sinks, conv_w, w_branch_attn, w_branch_conv, w_out):
    B, T, _ = x.shape
    cuts = np.cumsum([D_Q, D_KV, D_KV, D_CONV, D_CONV, D_CONV, D_MODEL]).tolist()
    q, k, v, cb, cc, ch, ga, gc = jnp.split(x @ w_in, cuts, axis=-1)
    q = q.reshape(B, T, N_HEADS, HEAD_DIM)
    k = k.reshape(B, T, N_KV, HEAD_DIM)
    v = v.reshape(B, T, N_KV, HEAD_DIM)
    offset = 0 if past is None else PAST_LEN
    pos = offset + jnp.arange(T, dtype=jnp.int32)
    q = rope(q, pos)
    k = rope(k, pos)
    u = cc * ch
    if past is None:
        nb = T // BLOCK
        qb = q.reshape(B, nb, BLOCK, N_KV, GROUP, HEAD_DIM)
        kb = k.reshape(B, nb, BLOCK, N_KV, HEAD_DIM)
        vb = v.reshape(B, nb, BLOCK, N_KV, HEAD_DIM)
        k_band = jnp.concatenate([jnp.concatenate([jnp.zeros_like(kb[:, :1]), kb[:, :-1]], 1), kb], 2)
        v_band = jnp.concatenate([jnp.concatenate([jnp.zeros_like(vb[:, :1]), vb[:, :-1]], 1), vb], 2)
        q_pos = pos.reshape(nb, BLOCK)
        k_pos = jnp.concatenate([q_pos - BLOCK, q_pos], axis=-1)
        attn = sink_window_attention(qb, k_band, v_band, q_pos, k_pos, sinks)
        new_k, new_v = k[:, -WINDOW:], v[:, -WINDOW:]
        u_pad = jnp.concatenate([jnp.zeros((B, CONV_W - 1, D_CONV), u.dtype), u], axis=1)
    else:
        k_past, v_past, conv_past = past
        k_all = jnp.concatenate([k_past, k], axis=1)
        v_all = jnp.concatenate([v_past, v], axis=1)
        q_pos = pos[None, :]
        k_pos = (PAST_LEN - WINDOW + jnp.arange(WINDOW + T, dtype=jnp.int32))[None, :]
        attn = sink_window_attention(q.reshape(B, 1, T, N_KV, GROUP, HEAD_DIM),
                                     k_all[:, None], v_all[:, None], q_pos, k_pos, sinks)
        new_k, new_v = k_all[:, -WINDOW:], v_all[:, -WINDOW:]
        u_pad = jnp.concatenate([conv_past, u], axis=1)
    attn = attn.reshape(B, T, D_Q)
    conv = sum(conv_w[j] * u_pad[:, j:j + T] for j in range(CONV_W))
    y_conv = cb * conv
    merged = jax.nn.sigmoid(ga) * (attn @ w_branch_attn) + jax.nn.sigmoid(gc) * (y_conv @ w_branch_conv)
    return merged @ w_out, new_k, new_v, u_pad[:, -(CONV_W - 1):]


def run_trunk(x, caches, ln_g, ln_b, w_in, sinks, conv_w, w_branch_attn, w_branch_conv, w_out,
              ffn1_gu, ffn1_down, ffn2_gu, ffn2_down):
    ks, vs, cs = [], [], []
    for l in range(DEPTH):
        past = None if caches is None else (caches[0][l], caches[1][l], caches[2][l])
        x = layer_norm(ALPHA * x + 0.5 * swiglu(x, ffn1_gu[l], ffn1_down[l]), ln_g[l, 0], ln_b[l, 0])
        m, nk, nv, nc = token_mix(x, past, w_in[l], sinks[l], conv_w[l],
                                  w_branch_attn[l], w_branch_conv[l], w_out[l])
        x = layer_norm(ALPHA * x + m, ln_g[l, 1], ln_b[l, 1])
        x = layer_norm(ALPHA * x + 0.5 * swiglu(x, ffn2_gu[l], ffn2_down[l]), ln_g[l, 2], ln_b[l, 2])
        ks.append(nk)
        vs.append(nv)
        cs.append(nc)
    return x, jnp.stack(ks), jnp.stack(vs), jnp.stack(cs)


def setup_inputs(seed: int = 0) -> dict:
    key = jax.random.key(seed)
    ks = jax.random.split(key, 20)
    f32 = jnp.float32
    nrm = lambda k, shape, scale: jax.random.normal(k, shape, f32) * scale
    return {
        "x_prompt": nrm(ks[0], (BATCH, SEQ, D_MODEL), 1.0),
        "x_sample": nrm(ks[1], (DEC_BATCH, DEC_SEQ, D_MODEL), 1.0),
        "cache_k_win": nrm(ks[2], (DEPTH, DEC_BATCH, WINDOW, N_KV, HEAD_DIM), 1.0),
        "cache_v_win": nrm(ks[3], (DEPTH, DEC_BATCH, WINDOW, N_KV, HEAD_DIM), 1.0),
        "state_conv": nrm(ks[4], (DEPTH, DEC_BATCH, CONV_W - 1, D_CONV), 1.0),
        "ln_g": 1.0 + nrm(ks[5], (DEPTH, 3, D_MODEL), 0.02),
        "ln_b": nrm(ks[6], (DEPTH, 3, D_MODEL), 0.02),
        "w_in": nrm(ks[7], (DEPTH, D_MODEL, N_IN), D_MODEL ** -0.5),
        "sinks": nrm(ks[8], (DEPTH, N_HEADS), 0.5),
        "conv_w": nrm(ks[9], (DEPTH, CONV_W, D_CONV), CONV_W ** -0.5),
        "w_branch_attn": nrm(ks[10], (DEPTH, D_Q, D_MODEL), D_Q ** -0.5),
        "w_branch_conv": nrm(ks[11], (DEPTH, D_CONV, D_MODEL), D_CONV ** -0.5),
        "w_out": nrm(ks[12], (DEPTH, D_MODEL, D_MODEL), BETA * D_MODEL ** -0.5),
        "ffn1_gu": nrm(ks[13], (DEPTH, D_MODEL, 2 * D_FF), D_MODEL ** -0.5),
        "ffn1_down": nrm(ks[14], (DEPTH, D_FF, D_MODEL), BETA * D_FF ** -0.5),
        "ffn2_gu": nrm(ks[15], (DEPTH, D_MODEL, 2 * D_FF), D_MODEL ** -0.5),
        "ffn2_down": nrm(ks[16], (DEPTH, D_FF, D_MODEL), BETA * D_FF ** -0.5),
    }


def reference(x_prompt, x_sample, cache_k_win, cache_v_win, state_conv, ln_g, ln_b, w_in, sinks,
              conv_w, w_branch_attn, w_branch_conv, w_out, ffn1_gu, ffn1_down, ffn2_gu, ffn2_down):
    y_prompt, k_win_prompt, v_win_prompt, conv_prompt = run_trunk(
        x_prompt, None, ln_g, ln_b, w_in, sinks, conv_w, w_branch_attn, w_branch_conv, w_out,
        ffn1_gu, ffn1_down, ffn2_gu, ffn2_down)
    y_sample, k_win_sample, v_win_sample, conv_sample = run_trunk(
        x_sample, (cache_k_win, cache_v_win, state_conv), ln_g, ln_b, w_in, sinks, conv_w,
        w_branch_attn, w_branch_conv, w_out, ffn1_gu, ffn1_down, ffn2_gu, ffn2_down)
    return (y_prompt, y_sample, k_win_prompt, v_win_prompt, conv_prompt,
            k_win_sample, v_win_sample, conv_sample)
```

```python
import os
import numpy as np
import concourse.bass as bass
import concourse.mybir as mybir
from concourse.bass_utils import run_bass_kernel_spmd

F32 = mybir.dt.float32
BF16 = mybir.dt.bfloat16
AF = mybir.ActivationFunctionType
ALU = mybir.AluOpType
AX = mybir.AxisListType

DEPTH = int(os.environ.get("MK_DEPTH", "4"))
STAGE = int(os.environ.get("MK_STAGE", "99"))
DM = 2048
KC = 16
DFF = 5632
NIN = 8704
ALPHA = float(8.0 ** 0.25)
EPS = 1e-5
WB = 784
NSLOT = 4
NEG = -30000.0
ENABLE_MIX = True
SEGS = [(0, 12), (12, 24), (24, 36), (36, 44)]
OQ, OK_, OV, OCB, OCC, OCH, OGA, OGC = 0, 1024, 1280, 1536, 2560, 3584, 4608, 6656


class Op:
    __slots__ = ("eng", "fn", "reads", "writes", "dma", "idx", "sig", "sigval", "deps", "dmaval")


class Prog:
    def __init__(self):
        self.ops = []

    def op(self, eng, fn, reads=(), writes=(), dma=None):
        o = Op()
        o.eng, o.fn, o.reads, o.writes, o.dma = eng, fn, list(reads), list(writes), dma
        o.idx = len(self.ops)
        o.sig = False
        o.sigval = 0
        o.dmaval = 0
        o.deps = None
        self.ops.append(o)
        return o

    def analyze(self):
        state = {}
        last_dma = {}
        for o in self.ops:
            deps = {}

            def add(d, raw):
                if d is None or d is o:
                    return
                same = (d.dma is None and o.dma is None and d.eng == o.eng)
                if same and o.eng == "pe":
                    return
                deps[d.idx] = d

            for (buf, lo, hi) in o.reads:
                for g in range(lo, hi + 1):
                    st = state.get((buf, g))
                    if st is not None:
                        add(st[0], True)
                        if buf in ("psf", "psb"):
                            for r in st[1].values():
                                add(r, False)
            for (buf, lo, hi) in o.writes:
                for g in range(lo, hi + 1):
                    st = state.get((buf, g))
                    if st is not None:
                        add(st[0], False)
                        for r in st[1].values():
                            add(r, False)
                        for r in st[2]:
                            add(r, False)
            if o.dma is not None:
                p = last_dma.get(o.dma)
                if p is not None:
                    deps[p.idx] = p
                last_dma[o.dma] = o
            for (buf, lo, hi) in o.reads:
                for g in range(lo, hi + 1):
                    st = state.get((buf, g))
                    if st is None:
                        st = [None, {}, []]
                        state[(buf, g)] = st
                    if o.dma is None:
                        st[1][o.eng] = o
                    else:
                        st[2].append(o)
            for (buf, lo, hi) in o.writes:
                for g in range(lo, hi + 1):
                    state[(buf, g)] = [o, {}, []]
            best = {}
            out = []
            for d in deps.values():
                if d.dma is None:
                    b = best.get(d.eng)
                    if b is None or d.idx > b.idx:
                        best[d.eng] = d
                else:
                    out.append(d)
            for d in best.values():
                d.sig = True
                out.append(d)
            o.deps = out
        cnt = {}
        dcnt = {}
        for o in self.ops:
            if o.dma is not None:
                dcnt[o.dma] = dcnt.get(o.dma, 0) + 16
                o.dmaval = dcnt[o.dma]
            elif o.sig:
                cnt[o.eng] = cnt.get(o.eng, 0) + 1
                o.sigval = cnt[o.eng]
        self.dma_final = dcnt
        self.eng_final = cnt

    def emit(self, nc, block_engines, esem, dsem):
        per = {}
        for o in self.ops:
            per.setdefault(o.eng, []).append(o)
        for eng, ops in per.items():
            e = block_engines[eng]
            waited = {}
            for o in ops:
                for d in o.deps:
                    if d.dma is not None:
                        key, val = ("d", d.dma), d.dmaval
                        sem = dsem[d.dma]
                    else:
                        key, val = ("e", d.eng), d.sigval
                        sem = esem[d.eng]
                    if waited.get(key, 0) >= val:
                        continue
                    waited[key] = val
                    e.wait_ge(sem, val)
                ins = o.fn(e)
                if o.dma is not None:
                    ins.then_inc(dsem[o.dma], 16)
                elif o.sig:
                    ins.then_inc(esem[o.eng], 1)


class Buf:
    def __init__(self, name, ap3, es, n1, n2, gran=256):
        self.name, self.ap, self.es, self.n1, self.n2, self.gran = name, ap3, es, n1, n2, gran

    def reg(self, i0, i1, a, b):
        lo = (i0 * self.n2 + a) * self.es
        hi = ((i1 - 1) * self.n2 + b) * self.es - 1
        return (self.name, lo // self.gran, hi // self.gran)

    def s(self, i, a, b, p0=0, p1=128):
        return self.ap[p0:p1, i, a:b], self.reg(i, i + 1, a, b)

    def m(self, i0, i1, a, b, p0=0, p1=128):
        return self.ap[p0:p1, i0:i1, a:b], self.reg(i0, i1, a, b)


def col_tiles(c0, c1, samp):
    if samp:
        c1 = 784
    t = []
    a = c0
    while a < c1:
        b = min(a + 512, c1)
        t.append((a, b))
        a = b
    return t


def build_program():
    nc = bass.Bass("TRN2", target_bir_lowering=False)
    P = Prog()

    def din(name, shape, dt=F32):
        return nc.dram_tensor(name, list(shape), dt, kind="ExternalInput").ap()

    def dout(name, shape, dt=F32):
        return nc.dram_tensor(name, list(shape), dt, kind="ExternalOutput").ap()

    xin = din("xin", [2, 128, KC, WB])
    ck = din("ck", [DEPTH, 4, 128, 256])
    cv = din("cv", [DEPTH, 4, 128, 256])
    sc = din("sc", [DEPTH, 4, 2, 1024])
    lng = din("lng", [128, DEPTH * 3 * KC])
    lnb = din("lnb", [128, DEPTH * 3 * KC])
    w_in = din("w_in", [DEPTH, DM, NIN])
    sinks = din("sinks", [128, DEPTH * 16])
    convw = din("convw", [128, DEPTH * 3 * 8])
    wba = din("wba", [DEPTH, 1024, DM])
    wbc = din("wbc", [DEPTH, 1024, DM])
    wout = din("wout", [DEPTH, DM, DM])
    f1gu = din("f1gu", [DEPTH, DM, 2 * DFF])
    f1d = din("f1d", [DEPTH, DFF, DM])
    f2gu = din("f2gu", [DEPTH, DM, 2 * DFF])
    f2d = din("f2d", [DEPTH, DFF, DM])
    cst_cos = din("cst_cos", [2, 128, WB])
    cst_sin = din("cst_sin", [2, 128, WB])
    cst_mask = din("cst_mask", [128, 4, 256])
    cst_ident = din("cst_ident", [128, 128])
    cst_perm = din("cst_perm", [128, 128])
    cst_flag = din("cst_flag", [128, 1])

    y_own = dout("y_own", [1024, DM])
    y_smp = dout("y_smp", [16, DM])
    kwin = dout("kwin", [DEPTH, 128, 256])
    vwin = dout("vwin", [DEPTH, 128, 256])
    convp = dout("convp", [DEPTH, 2, 1024])
    ksmp = dout("ksmp", [DEPTH, 4, 128, 256])
    vsmp = dout("vsmp", [DEPTH, 4, 128, 256])
    csmp = dout("csmp", [DEPTH, 8, 1024])

    import contextlib
    es = contextlib.ExitStack()
    with es:
        def sb(name, shape, dt):
            return es.enter_context(nc.sbuf_tensor("s_" + name, list(shape), dt))

        R_t = sb("R", [128, KC, WB], F32)
        xb_t = sb("xb", [128, KC, WB], BF16)
        wr_t = sb("wring", [128, NSLOT, 16, 256], BF16)
        NBF = 27008
        NF = 5400
        abf_t = sb("abf", [128, NBF], BF16)
        af_t = sb("af", [128, NF], F32)
        cos_t = sb("cos", [128, 1, WB], F32)
        sin_t = sb("sin", [128, 1, WB], F32)
        mask_t = sb("mask", [128, 4, 256], F32)
        idf_t = sb("idf", [128, 1, 128], F32)
        idb_t = sb("idb", [128, 1, 128], BF16)
        perm_t = sb("perm", [128, 1, 128], F32)
        ones_t = sb("ones", [128, 1, 128], BF16)
        flag_t = sb("flag", [128, 1, 1], F32)
        lng_t = sb("lng", [128, 1, DEPTH * 3 * KC], F32)
        lnb_t = sb("lnb", [128, 1, DEPTH * 3 * KC], F32)
        lnga_t = sb("lnga", [128, 1, DEPTH * 3 * KC], F32)
        lnba_t = sb("lnba", [128, 1, DEPTH * 3 * KC], F32)
        snk_t = sb("snk", [128, 1, DEPTH * 16], F32)
        nsnk_t = sb("nsnk", [128, 1, DEPTH * 16], F32)
        cw_t = sb("cw", [128, 1, DEPTH * 24], F32)
        kst_t = sb("kst", [128, DEPTH * 4, 128], BF16)
        vst_t = sb("vst", [128, DEPTH, 256], BF16)
        ust_t = sb("ust", [128, DEPTH * 8, 2], F32)
        st_t = sb("stat", [128, 4, 64], F32)
        osb_t = sb("osb", [128, 2, 512], F32)
        uo_t = sb("uo", [128, 8, 10], F32)
        psf_t = es.enter_context(nc.psum_tensor("psf", [128, 6, 512], F32))
        psb_t = es.enter_context(nc.psum_tensor("psb", [128, 2, 1024], BF16))

        R = Buf("R", R_t, 4, KC, WB)
        XB = Buf("xb", xb_t, 2, KC, WB)
        COS = Buf("cos", cos_t, 4, 1, WB)
        SIN = Buf("sin", sin_t, 4, 1, WB)
        MASK = Buf("mask", mask_t, 4, 4, 256)
        IDF = Buf("idf", idf_t, 4, 1, 128)
        IDB = Buf("idb", idb_t, 2, 1, 128)
        PERM = Buf("perm", perm_t, 4, 1, 128)
        ONES = Buf("ones", ones_t, 2, 1, 128)
        FLAG = Buf("flag", flag_t, 4, 1, 1)
        LNG = Buf("lng", lng_t, 4, 1, 192)
        LNB = Buf("lnb", lnb_t, 4, 1, 192)
        LNGA = Buf("lnga", lnga_t, 4, 1, 192)
        LNBA = Buf("lnba", lnba_t, 4, 1, 192)
        SNK = Buf("snk", snk_t, 4, 1, 64)
        NSNK = Buf("nsnk", nsnk_t, 4, 1, 64)
        CW = Buf("cw", cw_t, 4, 1, 96)
        KST = Buf("kst", kst_t, 2, DEPTH * 4, 128)
        VST = Buf("vst", vst_t, 2, DEPTH, 256)
        UST = Buf("ust", ust_t, 4, DEPTH * 8, 2, gran=8)
        STAT = Buf("stat", st_t, 4, 4, 64, gran=16)
        OSB = Buf("osb", osb_t, 4, 2, 512)
        UO = Buf("uo", uo_t, 4, 8, 10, gran=8)
        PSF = Buf("psf", psf_t, 4, 6, 512, gran=2048)
        PSB = Buf("psb", psb_t, 2, 2, 1024, gran=2048)

        def abf(off, n1, n2):
            assert off + n1 * n2 <= NBF, (off, n1, n2)
            ap = abf_t[:, off:off + n1 * n2].rearrange("p (a b) -> p a b", b=n2)
            b = Buf("abf", ap, 2, n1, n2)
            base = off
            oreg = b.reg
            b.reg = lambda i0, i1, a, bb, _o=oreg, _b=base: (
                "abf", ((_b + i0 * n2 + a) * 2) // 256, ((_b + (i1 - 1) * n2 + bb) * 2 - 1) // 256)
            return b

        def af(off, n1, n2):
            assert off + n1 * n2 <= NF, (off, n1, n2)
            ap = af_t[:, off:off + n1 * n2].rearrange("p (a b) -> p a b", b=n2)
            b = Buf("af", ap, 4, n1, n2)
            base = off
            b.reg = lambda i0, i1, a, bb, _b=base: (
                "af", ((_b + i0 * n2 + a) * 4) // 256, ((_b + (i1 - 1) * n2 + bb) * 4 - 1) // 256)
            return b

        H = abf(0, 12, WB)
        YBF = abf(9408, 2, 512)
        YSQ = abf(10432, 2, 512)
        SG = af(0, 2, 512)
        MEAN = af(1024, 2, 512)
        RSTD = af(2048, 2, 512)
        VTMP = af(3072, 2, 512)
        PA = abf(0, 16, WB)
        QB = abf(0, 8, WB)
        KD = abf(6272, 4, WB)
        VB = abf(9408, 7, 256)
        VS = abf(11200, 4, 256)
        ATT = abf(12544, 8, WB)
        PB_ = abf(18816, 2, 1024)
        PT = abf(20864, 2, 1024)
        KCT = abf(22912, 16, 128)
        ATOK = abf(24960, 1, 1024)
        VCB = abf(25984, 4, 256)
        YC = abf(18816, 8, WB)
        KF = af(0, 2, 512)
        RT = af(1024, 2, 512)
        SM = af(2048, 2, 1024)
        K32 = af(4096, 4, 144)
        CKD = af(4672, 4, 128)
        UU = af(0, 2, 800)
        CBF = af(1600, 1, WB)
        CCF = af(2384, 1, WB)
        SG2 = af(3168, 2, 512)
        TM2 = af(4192, 2, 512)

        rr = {"w": 0, "pf": 0, "pb": 0, "sg": 0, "st": 0, "o": 0, "pp": 0, "uu": 0, "s4": 0}

        def nxt(k, n):
            v = rr[k]
            rr[k] = (v + 1) % n
            return v

        W_AP = [wr_t[:, s_] for s_ in range(NSLOT)]
        W_REG = [("wr", s_, s_) for s_ in range(NSLOT)]
        for e_ in range(3):
            off_ = 11456 + e_ * 4096
            W_AP.append(abf_t[:, off_:off_ + 4096].rearrange("p (k n) -> p k n", n=256))
            W_REG.append(("abf", (off_ * 2) // 256, ((off_ + 4096) * 2 - 1) // 256))
        rr["wf"] = 0

        def wtile(src_ap, nk, ncols, ffn_phase=False):
            s = nxt("wf", NSLOT + 3) if ffn_phase else nxt("w", NSLOT)
            reg = W_REG[s]
            dst = W_AP[s][:, 0:nk, 0:ncols]
            P.op("pool", lambda e, d=dst, a=src_ap: e.dma_start(out=d, in_=a), writes=[reg], dma=("w", s))
            return s, reg

        def wv(dram3, l, r0, nk, c0, ncols):
            return dram3[l, r0:r0 + nk * 128, c0:c0 + ncols].rearrange("(k p) n -> p k n", p=128)

        def mm(out, lhsT, rhs, start, stop, reads, writes):
            P.op("pe", lambda e: e.matmul(out, lhsT=lhsT, rhs=rhs, start=start, stop=stop), reads=reads, writes=writes)

        def psf():
            return nxt("pf", 6)

        def ld(eng, dst_ap, dst_reg, src_ap, key):
            P.op(eng, lambda e: e.dma_start(out=dst_ap, in_=src_ap), writes=[dst_reg], dma=key)

        ld("sp", mask_t[:], MASK.reg(0, 4, 0, 256), cst_mask, ("c", 0))
        ld("sp", idf_t[:, 0, :], IDF.reg(0, 1, 0, 128), cst_ident, ("c", 1))
        ld("sp", perm_t[:, 0, :], PERM.reg(0, 1, 0, 128), cst_perm, ("c", 2))
        ld("sp", flag_t[:, 0, :], FLAG.reg(0, 1, 0, 1), cst_flag, ("c", 3))
        ld("sp", lng_t[:, 0, :], LNG.reg(0, 1, 0, 192), lng, ("c", 0))
        ld("sp", lnb_t[:, 0, :], LNB.reg(0, 1, 0, 192), lnb, ("c", 1))
        ld("sp", snk_t[:, 0, :], SNK.reg(0, 1, 0, 64), sinks, ("c", 2))
        ld("sp", cw_t[:, 0, :], CW.reg(0, 1, 0, 96), convw, ("c", 3))
        P.op("dve", lambda e: e.tensor_copy(out=idb_t[:, 0, :], in_=idf_t[:, 0, :]),
             reads=[IDF.reg(0, 1, 0, 128)], writes=[IDB.reg(0, 1, 0, 128)])
        P.op("dve", lambda e: e.memset(ones_t[:, 0, :], 1.0 / DM), writes=[ONES.reg(0, 1, 0, 128)])
        P.op("dve", lambda e: e.tensor_scalar_mul(out=lnga_t[:, 0, :], in0=lng_t[:, 0, :], scalar1=ALPHA),
             reads=[LNG.reg(0, 1, 0, 192)], writes=[LNGA.reg(0, 1, 0, 192)])
        P.op("dve", lambda e: e.tensor_scalar_mul(out=lnba_t[:, 0, :], in0=lnb_t[:, 0, :], scalar1=ALPHA),
             reads=[LNB.reg(0, 1, 0, 192)], writes=[LNBA.reg(0, 1, 0, 192)])
        P.op("dve", lambda e: e.tensor_scalar_mul(out=nsnk_t[:, 0, :], in0=snk_t[:, 0, :], scalar1=-1.0),
             reads=[SNK.reg(0, 1, 0, 64)], writes=[NSNK.reg(0, 1, 0, 64)])

        def ffn(l, gu, dn, tiles):
            for (c_lo, c_hi) in SEGS:
                nch = c_hi - c_lo
                for ci in range(c_lo, c_hi, 2):
                    sg_, rg = wtile(wv(gu, l, 0, 16, ci * 128, 256), 16, 256, True)
                    su_, ru = wtile(wv(gu, l, 0, 16, DFF + ci * 128, 256), 16, 256, True)
                    for (a, b) in tiles:
                        n = b - a
                        for hf in range(2):
                            bg, bu = psf(), psf()
                            pg, pgr = PSF.s(bg, 0, n)
                            pu, pur = PSF.s(bu, 0, n)
                            for k in range(KC):
                                x_ap, x_r = XB.s(k, a, b)
                                mm(pg, W_AP[sg_][:, k, hf * 128:(hf + 1) * 128], x_ap, k == 0, k == KC - 1, [rg, x_r], [pgr])
                            for k in range(KC):
                                x_ap, x_r = XB.s(k, a, b)
                                mm(pu, W_AP[su_][:, k, hf * 128:(hf + 1) * 128], x_ap, k == 0, k == KC - 1, [ru, x_r], [pur])
                            si = nxt("sg", 2)
                            s_ap, s_r = SG.s(si, 0, n)
                            P.op("act", lambda e, o=s_ap, i=pg: e.activation(out=o, in_=i, func=AF.Silu), reads=[pgr], writes=[s_r])
                            h_ap, h_r = H.s(ci - c_lo + hf, a, b)
                            P.op("dve", lambda e, o=h_ap, i0=s_ap, i1=pu: e.tensor_tensor(out=o, in0=i0, in1=i1, op=ALU.mult),
                                 reads=[s_r, pur], writes=[h_r])
                for j in range(0, KC, 2):
                    sd, rd = wtile(wv(dn, l, c_lo * 128, nch, j * 128, 256), nch, 256, True)
                    for (a, b) in tiles:
                        n = b - a
                        for hf in range(2):
                            bo = psf()
                            po, por = PSF.s(bo, 0, n)
                            for k in range(nch):
                                h_ap, h_r = H.s(k, a, b)
                                mm(po, W_AP[sd][:, k, hf * 128:(hf + 1) * 128], h_ap, k == 0, k == nch - 1, [rd, h_r], [por])
                            r_ap, r_r = R.s(j + hf, a, b)
                            P.op("dve", lambda e, o=r_ap, i=po: e.scalar_tensor_tensor(out=o, in0=i, scalar=0.5, in1=o, op0=ALU.mult, op1=ALU.add),
                                 reads=[por, r_r], writes=[r_r])

        def layernorm(l, which, tiles, final=False):
            pbase = (l * 3 + which) * KC
            for (a, b) in tiles:
                n = b - a
                bm, bq = psf(), psf()
                pm, pmr = PSF.s(bm, 0, n)
                pq, pqr = PSF.s(bq, 0, n)
                for k in range(KC):
                    r_ap, r_r = R.s(k, a, b)
                    i1 = nxt("st", 2)
                    yb, ybr = YBF.s(i1, 0, n)
                    ys, ysr = YSQ.s(i1, 0, n)
                    P.op("dve", lambda e, o=yb, i=r_ap: e.tensor_copy(out=o, in_=i), reads=[r_r], writes=[ybr])
                    P.op("act", lambda e, o=ys, i=r_ap: e.activation(out=o, in_=i, func=AF.Square), reads=[r_r], writes=[ysr])
                    mm(pm, ones_t[:, 0, :], yb, k == 0, k == KC - 1, [ONES.reg(0, 1, 0, 128), ybr], [pmr])
                    mm(pq, ones_t[:, 0, :], ys, k == 0, k == KC - 1, [ONES.reg(0, 1, 0, 128), ysr], [pqr])
                si = nxt("sg", 2)
                mn, mnr = MEAN.s(si, 0, n)
                rs, rsr = RSTD.s(si, 0, n)
                vt, vtr = VTMP.s(si, 0, n)
                P.op("act", lambda e, o=mn, i=pm: e.copy(out=o, in_=i), reads=[pmr], writes=[mnr])
                P.op("dve", lambda e, o=vt, i=mn: e.tensor_tensor(out=o, in0=i, in1=i, op=ALU.mult), reads=[mnr], writes=[vtr])
                P.op("dve", lambda e, o=vt, i=pq: e.tensor_tensor(out=o, in0=i, in1=o, op=ALU.subtract), reads=[pqr, vtr], writes=[vtr])
                P.op("dve", lambda e, o=vt: e.tensor_scalar_add(out=o, in0=o, scalar1=EPS), reads=[vtr], writes=[vtr])
                P.op("act", lambda e, o=vt: e.activation(out=o, in_=o, func=AF.Sqrt), reads=[vtr], writes=[vtr])
                P.op("dve", lambda e, o=rs, i=vt: e.reciprocal(out=o, in_=i), reads=[vtr], writes=[rsr])
                for k in range(KC):
                    r_ap, r_r = R.s(k, a, b)
                    x_ap, x_r = XB.s(k, a, b)
                    P.op("dve", lambda e, o=r_ap, i=mn: e.tensor_tensor(out=o, in0=o, in1=i, op=ALU.subtract), reads=[r_r, mnr], writes=[r_r])
                    P.op("dve", lambda e, o=r_ap, i=rs: e.tensor_tensor(out=o, in0=o, in1=i, op=ALU.mult), reads=[r_r, rsr], writes=[r_r])
                    g1 = lng_t[:, 0, pbase + k:pbase + k + 1]
                    b1 = lnb_t[:, 0, pbase + k:pbase + k + 1]
                    P.op("act", lambda e, o=x_ap, i=r_ap, g=g1, bb=b1: e.activation(out=o, in_=i, func=AF.Identity, bias=bb, scale=g),
                         reads=[r_r, LNG.reg(0, 1, 0, 192), LNB.reg(0, 1, 0, 192)], writes=[x_r])
                    if final:
                        g2, b2 = g1, b1
                        rg_, rb_ = LNG.reg(0, 1, 0, 192), LNB.reg(0, 1, 0, 192)
                    else:
                        g2 = lnga_t[:, 0, pbase + k:pbase + k + 1]
                        b2 = lnba_t[:, 0, pbase + k:pbase + k + 1]
                        rg_, rb_ = LNGA.reg(0, 1, 0, 192), LNBA.reg(0, 1, 0, 192)
                    P.op("act", lambda e, o=r_ap, g=g2, bb=b2: e.activation(out=o, in_=o, func=AF.Identity, bias=bb, scale=g),
                         reads=[r_r, rg_, rb_], writes=[r_r])

        def rope(ps_ap, ps_reg, a, b, out_ap, out_reg, k32=()):
            n = b - a
            si = nxt("sg", 2)
            kf, kfr = KF.s(si, 0, n)
            rt, rtr = RT.s(si, 0, n)
            P.op("act", lambda e: e.copy(out=kf, in_=ps_ap), reads=[ps_reg], writes=[kfr])
            br = psf()
            pr, prr = PSF.s(br, 0, n)
            mm(pr, perm_t[:, 0, :], kf, True, True, [PERM.reg(0, 1, 0, 128), kfr], [prr])
            c_ap, c_r = COS.s(0, a, b)
            s_ap, s_r = SIN.s(0, a, b)
            P.op("dve", lambda e: e.tensor_tensor(out=rt, in0=kf, in1=c_ap, op=ALU.mult), reads=[kfr, c_r], writes=[rtr])
            P.op("dve", lambda e: e.tensor_tensor(out=kf, in0=pr, in1=s_ap, op=ALU.mult), reads=[prr, s_r, kfr], writes=[kfr])
            P.op("dve", lambda e: e.tensor_tensor(out=out_ap, in0=rt, in1=kf, op=ALU.add), reads=[rtr, kfr], writes=[out_reg])
            for (o32, o32r, ca, cb) in k32:
                P.op("dve", lambda e, o=o32, x=RT.ap[:, si, ca:cb], y=KF.ap[:, si, ca:cb]: e.tensor_tensor(out=o, in0=x, in1=y, op=ALU.add),
                     reads=[rtr, kfr], writes=[o32r])

        def attn_unit(l, nq, qa, chunk, g, segs, mask_ap, Wt):
            bSs = [psf(), psf()]
            st = nxt("s4", 4)
            pp = nxt("pp", 2)
            sregs = [PSF.reg(bSs[0], bSs[0] + 1, 0, 512), PSF.reg(bSs[1], bSs[1] + 1, 0, 512)]
            streg = STAT.reg(st, st + 1, 0, 64)
            qreg = QB.reg(chunk, chunk + 1, qa, qa + nq)
            nseg = len(segs)
            for hh in range(2):
                pb = hh * 64
                off = 0
                for (kfn, v_ap, v_r, w) in segs:
                    k_ap, k_r = kfn(pb)
                    mm(psf_t[0:nq, bSs[hh], off:off + w], QB.ap[pb:pb + 64, chunk, qa:qa + nq], k_ap,
                       True, True, [qreg, k_r], [sregs[hh]])
                    off += w
            smr = SM.reg(pp, pp + 1, 0, 2 * Wt)
            for hh in range(2):
                P.op("dve", lambda e, hh=hh: e.tensor_tensor(out=SM.ap[0:nq, pp, hh * Wt:(hh + 1) * Wt],
                                                            in0=psf_t[0:nq, bSs[hh], 0:Wt], in1=mask_ap, op=ALU.add),
                     reads=[sregs[hh], MASK.reg(0, 4, 0, 256)], writes=[smr])
            sm3 = SM.ap[0:nq, pp, 0:2 * Wt].rearrange("p (h w) -> p h w", w=Wt)
            P.op("dve", lambda e: e.tensor_reduce(out=st_t[0:nq, st, 0:2], in_=sm3, axis=AX.X, op=ALU.max), reads=[smr], writes=[streg])
            P.op("dve", lambda e: e.memset(st_t[0:nq, st, 4:6], 0.0), writes=[streg])
            for hh in range(2):
                h = 2 * chunk + hh
                P.op("dve", lambda e, hh=hh, h=h: e.tensor_scalar(out=st_t[0:nq, st, 2 + hh:3 + hh], in0=st_t[0:nq, st, hh:hh + 1],
                                                                  scalar1=-0.125, scalar2=nsnk_t[0:nq, 0, l * 16 + h:l * 16 + h + 1],
                                                                  op0=ALU.mult, op1=ALU.min),
                     reads=[streg, NSNK.reg(0, 1, 0, 64)], writes=[streg])
            pbr = PB_.reg(pp, pp + 1, 0, 2 * Wt)
            for hh in range(2):
                h = 2 * chunk + hh
                P.op("act", lambda e, hh=hh: e.activation(out=PB_.ap[0:nq, pp, hh * Wt:(hh + 1) * Wt], in_=SM.ap[0:nq, pp, hh * Wt:(hh + 1) * Wt],
                                                          func=AF.Exp, bias=st_t[0:nq, st, 2 + hh:3 + hh], scale=0.125,
                                                          accum_out=st_t[0:nq, st, 4 + hh:5 + hh]),
                     reads=[smr, streg], writes=[pbr, streg])
                P.op("act", lambda e, hh=hh, h=h: e.activation(out=st_t[0:nq, st, 6 + hh:7 + hh], in_=snk_t[0:nq, 0, l * 16 + h:l * 16 + h + 1],
                                                               func=AF.Exp, bias=st_t[0:nq, st, 2 + hh:3 + hh], scale=1.0),
                     reads=[streg, SNK.reg(0, 1, 0, 64)], writes=[streg])
            P.op("dve", lambda e: e.tensor_tensor(out=st_t[0:nq, st, 8:10], in0=st_t[0:nq, st, 4:6], in1=st_t[0:nq, st, 6:8], op=ALU.add),
                 reads=[streg], writes=[streg])
            P.op("dve", lambda e: e.reciprocal(out=st_t[0:nq, st, 10:12], in_=st_t[0:nq, st, 8:10]), reads=[streg], writes=[streg])
            bT = nxt("pb", 2)
            treg = PSB.reg(bT, bT + 1, 0, 1024)
            for hh in range(2):
                off = 0
                for si, (kfn, v_ap, v_r, w) in enumerate(segs):
                    P.op("pe", lambda e, hh=hh, si=si, off=off, w=w: e.transpose(
                        psb_t[0:w, bT, (hh * nseg + si) * 128:(hh * nseg + si) * 128 + nq],
                        PB_.ap[0:nq, pp, hh * Wt + off:hh * Wt + off + w], idb_t[0:nq, 0, 0:nq]),
                        reads=[pbr, IDB.reg(0, 1, 0, 128)], writes=[treg])
                    off += w
            ptreg = PT.reg(pp, pp + 1, 0, 512)
            ncol = nseg * 2 * 128
            P.op("act", lambda e: e.copy(out=PT.ap[:, pp, 0:ncol], in_=psb_t[:, bT, 0:ncol]), reads=[treg], writes=[ptreg])
            bO = psf()
            oreg = PSF.reg(bO, bO + 1, 0, 512)
            for hh in range(2):
                for si, (kfn, v_ap, v_r, w) in enumerate(segs):
                    mm(psf_t[0:nq, bO, hh * 64:(hh + 1) * 64], PT.ap[0:w, pp, (hh * nseg + si) * 128:(hh * nseg + si) * 128 + nq], v_ap,
                       si == 0, si == nseg - 1, [ptreg, v_r], [oreg])
            for hh in range(2):
                h = 2 * chunk + hh
                P.op("act", lambda e, hh=hh, h=h: e.activation(out=ATOK.ap[0:nq, 0, h * 64:(h + 1) * 64], in_=psf_t[0:nq, bO, hh * 64:(hh + 1) * 64],
                                                               func=AF.Identity, scale=st_t[0:nq, st, 10 + hh:11 + hh]),
                     reads=[oreg, streg], writes=[ATOK.reg(0, 1, h * 64, (h + 1) * 64)])

        def attn_finish(nq, qa):
            bT = nxt("pb", 2)
            treg = PSB.reg(bT, bT + 1, 0, 1024)
            for c in range(8):
                P.op("pe", lambda e, c=c: e.transpose(psb_t[:, bT, c * 128:c * 128 + nq], ATOK.ap[0:nq, 0, c * 128:(c + 1) * 128], idb_t[0:nq, 0, 0:nq]),
                     reads=[ATOK.reg(0, 1, c * 128, (c + 1) * 128), IDB.reg(0, 1, 0, 128)], writes=[treg])
            src = psb_t[:, bT, :].rearrange("p (c n) -> p c n", n=128)[:, :, 0:nq]
            P.op("dve", lambda e: e.tensor_copy(out=ATT.ap[:, 0:8, qa:qa + nq], in_=src), reads=[treg], writes=[ATT.reg(0, 8, qa, qa + nq)])

        def MIX(l, ps_, c0, tiles, tiles_r):
            samp = (ps_ == 1)
            first = c0 // 128
            blocks = list(range(first, 6))
            for t2 in range(2):
                s_ = nxt("w", NSLOT)
                reg = ("wr", s_, s_)
                for g2 in range(2):
                    for dup in range(2):
                        col = OK_ + (2 * t2 + g2) * 64
                        src = w_in[l, :, col:col + 64].rearrange("(k p) n -> p k n", p=128)
                        dst = wr_t[:, s_, :, g2 * 128 + dup * 64:g2 * 128 + dup * 64 + 64]
                        P.op("pool", lambda e, d=dst, a=src: e.dma_start(out=d, in_=a), writes=[reg], dma=("w", s_))
                for (a, b) in tiles:
                    n = b - a
                    for g2 in range(2):
                        g = 2 * t2 + g2
                        bk = psf()
                        pk, pkr = PSF.s(bk, 0, n)
                        for k in range(KC):
                            x_ap, x_r = XB.s(k, a, b)
                            mm(pk, wr_t[:, s_, k, g2 * 128:(g2 + 1) * 128], x_ap, k == 0, k == KC - 1, [reg, x_r], [pkr])
                        o_ap, o_r = KD.s(g, a, b)
                        k32 = []
                        if samp:
                            for (r_lo, r_hi, d_off) in ((640, 768, 0), (768, 784, 128)):
                                lo, hi = max(a, r_lo), min(b, r_hi)
                                if lo < hi:
                                    d_ap, d_r = K32.s(g, d_off + lo - r_lo, d_off + hi - r_lo)
                                    k32.append((d_ap, d_r, lo - a, hi - a))
                        rope(pk, pkr, a, b, o_ap, o_r, k32)
            if not samp:
                for g in range(4):
                    s_ap, s_r = KD.s(g, 640, 768)
                    d_ap, d_r = KST.s(l * 4 + g, 0, 128)
                    P.op("dve", lambda e, o=d_ap, i=s_ap: e.tensor_copy(out=o, in_=i), reads=[s_r], writes=[d_r])
            else:
                bo = psf()
                for g in range(4):
                    i_ap, i_r = K32.s(g, 0, 128)
                    o_ap, o_r = PSF.s(bo, g * 128, (g + 1) * 128)
                    P.op("pe", lambda e, o=o_ap, i=i_ap: e.transpose(o, i, idf_t[:, 0, :]), reads=[i_r, IDF.reg(0, 1, 0, 128)], writes=[o_r])
                oi = nxt("o", 2)
                P.op("act", lambda e, oi=oi, bo=bo: e.copy(out=osb_t[:, oi, 0:256].rearrange("p (g e) -> p g e", e=64),
                                                       in_=psf_t[:, bo, :].rearrange("p (g e) -> p g e", e=128)[:, :, 0:64]),
                     reads=[PSF.reg(bo, bo + 1, 0, 512)], writes=[OSB.reg(oi, oi + 1, 0, 512)])
                store(kwin[l], osb_t[:, oi, 0:256], OSB.reg(oi, oi + 1, 0, 512))
                bo = psf()
                for g in range(4):
                    i_ap, i_r = K32.s(g, 128, 144)
                    P.op("pe", lambda e, g=g, i=i_ap, bo=bo: e.transpose(psf_t[0:16, bo, g * 128:(g + 1) * 128], i, idf_t[:, 0, :]),
                         reads=[i_r, IDF.reg(0, 1, 0, 128)], writes=[PSF.reg(bo, bo + 1, 0, 512)])
                oi = nxt("o", 2)
                P.op("act", lambda e, oi=oi, bo=bo: e.copy(out=osb_t[0:16, oi, 0:256].rearrange("p (g e) -> p g e", e=64),
                                                       in_=psf_t[0:16, bo, :].rearrange("p (g e) -> p g e", e=128)[:, :, 0:64]),
                     reads=[PSF.reg(bo, bo + 1, 0, 512)], writes=[OSB.reg(oi, oi + 1, 0, 512)])
                for s4 in range(4):
                    store(ksmp[l, s4, 124:128, :], osb_t[4 * s4:4 * s4 + 4, oi, 0:256], OSB.reg(oi, oi + 1, 0, 512))
            if STAGE < 4:
                return
            for t4 in range(4):
                sq, rq = wtile(wv(w_in, l, 0, 16, OQ + t4 * 256, 256), 16, 256)
                for (a, b) in tiles_r:
                    n = b - a
                    for hf in range(2):
                        bq_ = psf()
                        pq_, pqr_ = PSF.s(bq_, 0, n)
                        for k in range(KC):
                            x_ap, x_r = XB.s(k, a, b)
                            mm(pq_, wr_t[:, sq, k, hf * 128:(hf + 1) * 128], x_ap, k == 0, k == KC - 1, [rq, x_r], [pqr_])
                        o_ap, o_r = QB.s(2 * t4 + hf, a, b)
                        rope(pq_, pqr_, a, b, o_ap, o_r)
            if STAGE < 5:
                return
            sv, rv = wtile(wv(w_in, l, 0, 16, OV, 256), 16, 256)
            for nb_ in blocks:
                ca = nb_ * 128
                bv = psf()
                pv, pvr = PSF.s(bv, 0, 256)
                for k in range(KC):
                    mm(pv, XB.ap[:, k, ca:ca + 128], wr_t[:, sv, k, 0:256], k == 0, k == KC - 1, [XB.reg(k, k + 1, ca, ca + 128), rv], [pvr])
                v_ap, v_r = VB.s(nb_, 0, 256)
                P.op("act", lambda e, o=v_ap, i=pv: e.copy(out=o, in_=i), reads=[pvr], writes=[v_r])
                if nb_ == 5 and not samp:
                    d_ap, d_r = VST.s(l, 0, 256)
                    P.op("dve", lambda e, o=d_ap, i=v_ap: e.tensor_copy(out=o, in_=i), reads=[v_r], writes=[d_r])
                if nb_ == 5 and samp:
                    oi = nxt("o", 2)
                    P.op("dve", lambda e, oi=oi, i=pv: e.tensor_copy(out=osb_t[:, oi, 0:256], in_=i), reads=[pvr], writes=[OSB.reg(oi, oi + 1, 0, 512)])
                    store(vwin[l], osb_t[:, oi, 0:256], OSB.reg(oi, oi + 1, 0, 512))
            if samp:
                for s4 in range(4):
                    qa = 768 + 4 * s4
                    bv = psf()
                    pvr = PSF.reg(bv, bv + 1, 0, 512)
                    for k in range(KC):
                        mm(psf_t[0:4, bv, 0:256], XB.ap[:, k, qa:qa + 4], wr_t[:, sv, k, 0:256], k == 0, k == KC - 1,
                           [XB.reg(k, k + 1, qa, qa + 4), rv], [pvr])
                    P.op("act", lambda e, s4=s4, bv=bv: e.copy(out=VS.ap[0:4, s4, :], in_=psf_t[0:4, bv, 0:256]), reads=[pvr], writes=[VS.reg(s4, s4 + 1, 0, 256)])
                    oi = nxt("o", 2)
                    P.op("dve", lambda e, oi=oi, bv=bv: e.tensor_copy(out=osb_t[0:4, oi, 0:256], in_=psf_t[0:4, bv, 0:256]), reads=[pvr],
                         writes=[OSB.reg(oi, oi + 1, 0, 512)])
                    store(vsmp[l, s4, 124:128, :], osb_t[0:4, oi, 0:256], OSB.reg(oi, oi + 1, 0, 512))
                for s4 in range(4):
                    srck = ck[l, s4].rearrange("t (g e) -> t g e", e=64)
                    for dup in range(2):
                        ld("sp", CKD.ap[:, :, dup * 64:(dup + 1) * 64], CKD.reg(0, 4, 0, 128), srck, ("k", dup))
                    bo = psf()
                    for g in range(4):
                        i_ap, i_r = CKD.s(g, 0, 128)
                        o_ap, o_r = PSF.s(bo, g * 128, (g + 1) * 128)
                        P.op("pe", lambda e, o=o_ap, i=i_ap: e.transpose(o, i, idf_t[:, 0, :]), reads=[i_r, IDF.reg(0, 1, 0, 128)], writes=[o_r])
                    P.op("act", lambda e, s4=s4, bo=bo: e.copy(out=KCT.ap[:, s4 * 4:s4 * 4 + 4, :], in_=psf_t[:, bo, :].rearrange("p (g e) -> p g e", e=128)),
                         reads=[PSF.reg(bo, bo + 1, 0, 512)], writes=[KCT.reg(s4 * 4, s4 * 4 + 4, 0, 128)])
                    sm_ap, sm_r = SM.s(s4 % 2, 0, 256)
                    ld("sp", sm_ap, sm_r, cv[l, s4], ("k", 2))
                    vc_ap, vc_r = VCB.s(s4, 0, 256)
                    P.op("dve", lambda e, o=vc_ap, i=sm_ap: e.tensor_copy(out=o, in_=i), reads=[sm_r], writes=[vc_r])
                    P.op("sp", lambda e, s4=s4: e.dma_start(out=ksmp[l, s4, 0:124, :], in_=ck[l, s4, 4:128, :]), dma=("s", st_keys["n"] % 4))
                    st_keys["n"] += 1
                    P.op("sp", lambda e, s4=s4: e.dma_start(out=vsmp[l, s4, 0:124, :], in_=cv[l, s4, 4:128, :]), dma=("s", st_keys["n"] % 4))
                    st_keys["n"] += 1
            if STAGE < 6:
                return
            for nb_ in (blocks if samp else blocks[1:]):
                ca = nb_ * 128
                for chunk in range(8):
                    g = chunk // 2
                    segs = []
                    if nb_ > first:
                        segs.append((lambda pb, g=g, ca=ca: (KD.ap[pb:pb + 64, g, ca - 128:ca], KD.reg(g, g + 1, ca - 128, ca)),
                                     VB.ap[:, nb_ - 1, g * 64:(g + 1) * 64], VB.reg(nb_ - 1, nb_, 0, 256), 128))
                    elif samp:
                        segs.append((lambda pb, g=g: (KST.ap[pb:pb + 64, l * 4 + g, 0:128], KST.reg(l * 4 + g, l * 4 + g + 1, 0, 128)),
                                     VST.ap[:, l, g * 64:(g + 1) * 64], VST.reg(l, l + 1, 0, 256), 128))
                    segs.append((lambda pb, g=g, ca=ca: (KD.ap[pb:pb + 64, g, ca:ca + 128], KD.reg(g, g + 1, ca, ca + 128)),
                                 VB.ap[:, nb_, g * 64:(g + 1) * 64], VB.reg(nb_, nb_ + 1, 0, 256), 128))
                    if len(segs) == 1:
                        m_ap, Wt = mask_t[:, 0, 128:256], 128
                    elif (not samp) and nb_ == 4:
                        m_ap, Wt = mask_t[:, 1, 0:256], 256
                    else:
                        m_ap, Wt = mask_t[:, 0, 0:256], 256
                    attn_unit(l, 128, ca, chunk, g, segs, m_ap, Wt)
                attn_finish(128, ca)
            if samp:
                for s4 in range(4):
                    qa = 768 + 4 * s4
                    for chunk in range(8):
                        g = chunk // 2
                        segs = [
                            (lambda pb, g=g, s4=s4: (KCT.ap[pb:pb + 64, s4 * 4 + g, 0:128], KCT.reg(s4 * 4 + g, s4 * 4 + g + 1, 0, 128)),
                             VCB.ap[:, s4, g * 64:(g + 1) * 64], VCB.reg(s4, s4 + 1, 0, 256), 128),
                            (lambda pb, g=g, qa=qa: (KD.ap[pb:pb + 64, g, qa:qa + 4], KD.reg(g, g + 1, qa, qa + 4)),
                             VS.ap[0:4, s4, g * 64:(g + 1) * 64], VS.reg(s4, s4 + 1, 0, 256), 4),
                        ]
                        attn_unit(l, 4, qa, chunk, g, segs, mask_t[0:4, 2, 0:132], 132)
                    attn_finish(4, qa)
            if STAGE < 7:
                return
            for j in range(0, KC, 2):
                sga, rga = wtile(wv(w_in, l, 0, 16, OGA + j * 128, 256), 16, 256)
                swa, rwa = wtile(wv(wba, l, 0, 8, j * 128, 256), 8, 256)
                for (a, b) in tiles_r:
                    n = b - a
                    for hf in range(2):
                        bg, ba_ = psf(), psf()
                        pg, pgr = PSF.s(bg, 0, n)
                        pa, par = PSF.s(ba_, 0, n)
                        for k in range(KC):
                            x_ap, x_r = XB.s(k, a, b)
                            mm(pg, wr_t[:, sga, k, hf * 128:(hf + 1) * 128], x_ap, k == 0, k == KC - 1, [rga, x_r], [pgr])
                        for k in range(8):
                            t_ap, t_r = ATT.s(k, a, b)
                            mm(pa, wr_t[:, swa, k, hf * 128:(hf + 1) * 128], t_ap, k == 0, k == 7, [rwa, t_r], [par])
                        si = nxt("sg", 2)
                        s_ap, s_r = SG2.s(si, 0, n)
                        P.op("act", lambda e, o=s_ap, i=pg: e.activation(out=o, in_=i, func=AF.Sigmoid), reads=[pgr], writes=[s_r])
                        o_ap, o_r = PA.s(j + hf, a, b)
                        P.op("dve", lambda e, o=o_ap, i0=s_ap, i1=pa: e.tensor_tensor(out=o, in0=i0, in1=i1, op=ALU.mult), reads=[s_r, par], writes=[o_r])
            if STAGE < 8:
                return
            if samp:
                srcs = sc[l].rearrange("s t c -> (s t) c")
                for h2 in range(2):
                    ld("sp", osb_t[0:8, h2, :], OSB.reg(h2, h2 + 1, 0, 512), srcs[:, h2 * 512:(h2 + 1) * 512], ("k", h2))
            for i2 in range(4):
                scc, rcc = wtile(wv(w_in, l, 0, 16, OCC + i2 * 256, 256), 16, 256)
                sch, rch = wtile(wv(w_in, l, 0, 16, OCH + i2 * 256, 256), 16, 256)
                scb, rcb = wtile(wv(w_in, l, 0, 16, OCB + i2 * 256, 256), 16, 256)
                for hf in range(2):
                    i = 2 * i2 + hf
                    ui = nxt("uu", 2)
                    uall = UU.reg(ui, ui + 1, 0, 800)
                    usv = UU.ap[:, ui, 770:794].rearrange("p (s t) -> p s t", t=6)
                    if not samp:
                        P.op("dve", lambda e, ui=ui: e.memset(UU.ap[:, ui, c0:c0 + 2], 0.0), writes=[uall])
                    else:
                        P.op("dve", lambda e, ui=ui, i=i: e.tensor_copy(out=UU.ap[:, ui, 0:2], in_=ust_t[:, l * 8 + i, :]),
                             reads=[UST.reg(l * 8 + i, l * 8 + i + 1, 0, 2)], writes=[uall])
                        bo = psf()
                        P.op("pe", lambda e, i=i, bo=bo: e.transpose(psf_t[:, bo, 0:8], osb_t[0:8, i // 4, (i % 4) * 128:(i % 4 + 1) * 128], idf_t[0:8, 0, 0:8]),
                             reads=[OSB.reg(i // 4, i // 4 + 1, 0, 512), IDF.reg(0, 1, 0, 128)], writes=[PSF.reg(bo, bo + 1, 0, 512)])
                        P.op("dve", lambda e, usv=usv, bo=bo: e.tensor_copy(out=usv[:, :, 0:2], in_=psf_t[:, bo, 0:8].rearrange("p (s t) -> p s t", t=2)),
                             reads=[PSF.reg(bo, bo + 1, 0, 512)], writes=[uall])
                    for (a, b) in tiles:
                        n = b - a
                        b1, b2, b3 = psf(), psf(), psf()
                        pcc, pccr = PSF.s(b1, 0, n)
                        pch, pchr = PSF.s(b2, 0, n)
                        pcb, pcbr = PSF.s(b3, 0, n)
                        for (pp_, ppr_, sl_, rg_) in ((pcc, pccr, scc, rcc), (pch, pchr, sch, rch), (pcb, pcbr, scb, rcb)):
                            for k in range(KC):
                                x_ap, x_r = XB.s(k, a, b)
                                mm(pp_, wr_t[:, sl_, k, hf * 128:(hf + 1) * 128], x_ap, k == 0, k == KC - 1, [rg_, x_r], [ppr_])
                        cc_ap, cc_r = CCF.s(0, a, b)
                        cb_ap, cb_r = CBF.s(0, a, b)
                        P.op("act", lambda e, o=cc_ap, i=pcc: e.copy(out=o, in_=i), reads=[pccr], writes=[cc_r])
                        P.op("act", lambda e, o=cb_ap, i=pcb: e.copy(out=o, in_=i), reads=[pcbr], writes=[cb_r])
                        pe_ = min(b, 768)
                        if a < pe_:
                            P.op("dve", lambda e, ui=ui, a=a, pe_=pe_, b2=b2: e.tensor_tensor(
                                out=UU.ap[:, ui, 2 + a:2 + pe_], in0=CCF.ap[:, 0, a:pe_], in1=psf_t[:, b2, 0:pe_ - a], op=ALU.mult),
                                reads=[cc_r, pchr], writes=[uall])
                        if b > 768:
                            so = 768 - a
                            P.op("dve", lambda e, usv=usv, so=so, b2=b2: e.tensor_tensor(
                                out=usv[:, :, 2:6], in0=CCF.ap[:, 0, 768:784].rearrange("p (s t) -> p s t", t=4),
                                in1=psf_t[:, b2, so:so + 16].rearrange("p (s t) -> p s t", t=4), op=ALU.mult),
                                reads=[cc_r, pchr], writes=[uall])
                    if not samp:
                        P.op("dve", lambda e, ui=ui: e.tensor_scalar_mul(out=UU.ap[:, ui, 512:514], in0=UU.ap[:, ui, 512:514], scalar1=flag_t[:, 0, 0:1]),
                             reads=[uall, FLAG.reg(0, 1, 0, 1)], writes=[uall])
                    w0 = cw_t[:, 0, (l * 3 + 0) * 8 + i:(l * 3 + 0) * 8 + i + 1]
                    w1 = cw_t[:, 0, (l * 3 + 1) * 8 + i:(l * 3 + 1) * 8 + i + 1]
                    w2 = cw_t[:, 0, (l * 3 + 2) * 8 + i:(l * 3 + 2) * 8 + i + 1]
                    cwr = CW.reg(0, 1, 0, 96)
                    ccall = CCF.reg(0, 1, 0, WB)
                    cball = CBF.reg(0, 1, 0, WB)
                    acc = CCF.ap[:, 0, c0:768]
                    P.op("dve", lambda e, ui=ui, acc=acc, w0=w0: e.tensor_scalar_mul(out=acc, in0=UU.ap[:, ui, c0:768], scalar1=w0),
                         reads=[uall, cwr], writes=[ccall])
                    P.op("dve", lambda e, ui=ui, acc=acc, w1=w1: e.scalar_tensor_tensor(out=acc, in0=UU.ap[:, ui, c0 + 1:769], scalar=w1, in1=acc, op0=ALU.mult, op1=ALU.add),
                         reads=[uall, cwr, ccall], writes=[ccall])
                    P.op("dve", lambda e, ui=ui, acc=acc, w2=w2: e.scalar_tensor_tensor(out=acc, in0=UU.ap[:, ui, c0 + 2:770], scalar=w2, in1=acc, op0=ALU.mult, op1=ALU.add),
                         reads=[uall, cwr, ccall], writes=[ccall])
                    y_ap, y_r = YC.s(i, c0, 768)
                    P.op("dve", lambda e, o=y_ap, acc=acc: e.tensor_tensor(out=o, in0=acc, in1=CBF.ap[:, 0, c0:768], op=ALU.mult),
                         reads=[ccall, cball], writes=[y_r])
                    if samp:
                        accs = CCF.ap[:, 0, 768:784].rearrange("p (s t) -> p s t", t=4)
                        cbs = CBF.ap[:, 0, 768:784].rearrange("p (s t) -> p s t", t=4)
                        P.op("dve", lambda e, usv=usv, accs=accs, w0=w0: e.tensor_scalar_mul(out=accs, in0=usv[:, :, 0:4], scalar1=w0),
                             reads=[uall, cwr], writes=[ccall])
                        P.op("dve", lambda e, usv=usv, accs=accs, w1=w1: e.scalar_tensor_tensor(out=accs, in0=usv[:, :, 1:5], scalar=w1, in1=accs, op0=ALU.mult, op1=ALU.add),
                             reads=[uall, cwr, ccall], writes=[ccall])
                        P.op("dve", lambda e, usv=usv, accs=accs, w2=w2: e.scalar_tensor_tensor(out=accs, in0=usv[:, :, 2:6], scalar=w2, in1=accs, op0=ALU.mult, op1=ALU.add),
                             reads=[uall, cwr, ccall], writes=[ccall])
                        ys_ap, ys_r = YC.s(i, 768, 784)
                        P.op("dve", lambda e, o=ys_ap, accs=accs, cbs=cbs: e.tensor_tensor(out=o.rearrange("p (s t) -> p s t", t=4), in0=accs, in1=cbs, op=ALU.mult),
                             reads=[ccall, cball], writes=[ys_r])
                        P.op("dve", lambda e, ui=ui, i=i: e.tensor_copy(out=uo_t[:, i, 0:2], in_=UU.ap[:, ui, 768:770]), reads=[uall], writes=[UO.reg(i, i + 1, 0, 10)])
                        P.op("dve", lambda e, usv=usv, i=i: e.tensor_copy(out=uo_t[:, i, 2:10].rearrange("p (s t) -> p s t", t=2), in_=usv[:, :, 4:6]),
                             reads=[uall], writes=[UO.reg(i, i + 1, 0, 10)])
                    else:
                        P.op("dve", lambda e, ui=ui, i=i: e.tensor_copy(out=ust_t[:, l * 8 + i, :], in_=UU.ap[:, ui, 768:770]), reads=[uall],
                             writes=[UST.reg(l * 8 + i, l * 8 + i + 1, 0, 2)])
            if samp:
                for (c_lo, c_hi, nrow, dst) in ((0, 2, 2, convp[l]), (2, 10, 8, csmp[l])):
                    for h2 in range(2):
                        bo = psf()
                        for ii in range(4):
                            i = h2 * 4 + ii
                            P.op("pe", lambda e, i=i, ii=ii, bo=bo, c_lo=c_lo, c_hi=c_hi, nrow=nrow: e.transpose(
                                psf_t[0:nrow, bo, ii * 128:(ii + 1) * 128], uo_t[:, i, c_lo:c_hi], idf_t[:, 0, :]),
                                reads=[UO.reg(i, i + 1, 0, 10), IDF.reg(0, 1, 0, 128)], writes=[PSF.reg(bo, bo + 1, 0, 512)])
                        oi = nxt("o", 2)
                        P.op("act", lambda e, oi=oi, bo=bo, nrow=nrow: e.copy(out=osb_t[0:nrow, oi, :], in_=psf_t[0:nrow, bo, :]),
                             reads=[PSF.reg(bo, bo + 1, 0, 512)], writes=[OSB.reg(oi, oi + 1, 0, 512)])
                        store(dst[:, h2 * 512:(h2 + 1) * 512], osb_t[0:nrow, oi, :], OSB.reg(oi, oi + 1, 0, 512))
            if STAGE < 9:
                return
            for j in range(0, KC, 2):
                sgc, rgc = wtile(wv(w_in, l, 0, 16, OGC + j * 128, 256), 16, 256)
                swc, rwc = wtile(wv(wbc, l, 0, 8, j * 128, 256), 8, 256)
                for (a, b) in tiles_r:
                    n = b - a
                    for hf in range(2):
                        bg, bc_ = psf(), psf()
                        pg, pgr = PSF.s(bg, 0, n)
                        pc, pcr = PSF.s(bc_, 0, n)
                        for k in range(KC):
                            x_ap, x_r = XB.s(k, a, b)
                            mm(pg, wr_t[:, sgc, k, hf * 128:(hf + 1) * 128], x_ap, k == 0, k == KC - 1, [rgc, x_r], [pgr])
                        for k in range(8):
                            t_ap, t_r = YC.s(k, a, b)
                            mm(pc, wr_t[:, swc, k, hf * 128:(hf + 1) * 128], t_ap, k == 0, k == 7, [rwc, t_r], [pcr])
                        si = nxt("sg", 2)
                        s_ap, s_r = SG2.s(si, 0, n)
                        t_ap2, t_r2 = TM2.s(si, 0, n)
                        P.op("act", lambda e, o=s_ap, i=pg: e.activation(out=o, in_=i, func=AF.Sigmoid), reads=[pgr], writes=[s_r])
                        P.op("dve", lambda e, o=t_ap2, i0=s_ap, i1=pc: e.tensor_tensor(out=o, in0=i0, in1=i1, op=ALU.mult), reads=[s_r, pcr], writes=[t_r2])
                        o_ap, o_r = PA.s(j + hf, a, b)
                        P.op("dve", lambda e, o=o_ap, i0=t_ap2: e.tensor_tensor(out=o, in0=i0, in1=o, op=ALU.add), reads=[t_r2, o_r], writes=[o_r])
            for j in range(0, KC, 2):
                so, ro = wtile(wv(wout, l, 0, 16, j * 128, 256), 16, 256)
                for (a, b) in tiles_r:
                    n = b - a
                    for hf in range(2):
                        bo = psf()
                        po, por = PSF.s(bo, 0, n)
                        for k in range(KC):
                            m_ap, m_r = PA.s(k, a, b)
                            mm(po, wr_t[:, so, k, hf * 128:(hf + 1) * 128], m_ap, k == 0, k == KC - 1, [ro, m_r], [por])
                        r_ap, r_r = R.s(j + hf, a, b)
                        P.op("dve", lambda e, o=r_ap, i=po: e.tensor_tensor(out=o, in0=i, in1=o, op=ALU.add), reads=[por, r_r], writes=[r_r])

        st_keys = {"n": 0}

        def store(dst_ap, src_ap, src_reg):
            k = st_keys["n"] % 4
            st_keys["n"] += 1
            P.op("sp", lambda e: e.dma_start(out=dst_ap, in_=src_ap), reads=[src_reg], dma=("s", k))

        for ps_ in range(2):
            samp = (ps_ == 1)
            ld("sp", R_t[:], R.reg(0, KC, 0, WB), xin[ps_], ("x", 0))
            ld("sp", cos_t[:, 0, :], COS.reg(0, 1, 0, WB), cst_cos[ps_], ("x", 1))
            ld("sp", sin_t[:, 0, :], SIN.reg(0, 1, 0, WB), cst_sin[ps_], ("x", 2))
            for k in range(KC):
                r_ap, r_r = R.s(k, 0, WB)
                x_ap, x_r = XB.s(k, 0, WB)
                P.op("act", lambda e, o=x_ap, i=r_ap: e.copy(out=o, in_=i), reads=[r_r], writes=[x_r])
                P.op("dve", lambda e, o=r_ap: e.tensor_scalar_mul(out=o, in0=o, scalar1=ALPHA), reads=[r_r], writes=[r_r])
            for l in range(DEPTH):
                c0 = (512 - 128 * (DEPTH - l)) if ps_ == 0 else 0
                tiles = col_tiles(c0, 768, samp)
                tiles_r = col_tiles(c0 + 128, 768, samp) if ps_ == 0 else tiles
                if STAGE >= 1:
                    ffn(l, f1gu, f1d, tiles)
                if STAGE >= 2:
                    layernorm(l, 0, tiles)
                if STAGE >= 3:
                    MIX(l, ps_, c0, tiles, tiles_r)
                if STAGE >= 10:
                    layernorm(l, 1, tiles_r)
                    ffn(l, f2gu, f2d, tiles_r)
                    layernorm(l, 2, tiles_r, final=(l == DEPTH - 1))
            blocks = list(range(4, 6)) if ps_ == 0 else list(range(0, 6))
            for nb_ in blocks:
                ca = nb_ * 128
                row0 = (nb_ - 4) * 128 if ps_ == 0 else 256 + nb_ * 128
                for q4 in range(4):
                    bo = psf()
                    for kk in range(4):
                        k = q4 * 4 + kk
                        r_ap, r_r = R.s(k, ca, ca + 128)
                        o_ap, o_r = PSF.s(bo, kk * 128, kk * 128 + 128)
                        P.op("pe", lambda e, o=o_ap, i=r_ap: e.transpose(o, i, idf_t[:, 0, :]),
                             reads=[r_r, IDF.reg(0, 1, 0, 128)], writes=[o_r])
                    oi = nxt("o", 2)
                    s_ap, s_r = OSB.s(oi, 0, 512)
                    p_ap, p_r = PSF.s(bo, 0, 512)
                    P.op("act", lambda e, o=s_ap, i=p_ap: e.copy(out=o, in_=i), reads=[p_r], writes=[s_r])
                    store(y_own[row0:row0 + 128, q4 * 512:(q4 + 1) * 512], s_ap, s_r)
            if samp:
                for q4 in range(4):
                    bo = psf()
                    for kk in range(4):
                        k = q4 * 4 + kk
                        r_ap, r_r = R.s(k, 768, 784)
                        o_ap = psf_t[0:16, bo, kk * 128:(kk + 1) * 128]
                        o_r = PSF.reg(bo, bo + 1, 0, 512)
                        P.op("pe", lambda e, o=o_ap, i=r_ap: e.transpose(o, i, idf_t[:, 0, :]),
                             reads=[r_r, IDF.reg(0, 1, 0, 128)], writes=[o_r])
                    oi = nxt("o", 2)
                    s_ap = osb_t[0:16, oi, :]
                    s_r = OSB.reg(oi, oi + 1, 0, 512)
                    p_ap = psf_t[0:16, bo, :]
                    P.op("act", lambda e, o=s_ap, i=p_ap: e.copy(out=o, in_=i), reads=[PSF.reg(bo, bo + 1, 0, 512)], writes=[s_r])
                    store(y_smp[:, q4 * 512:(q4 + 1) * 512], s_ap, s_r)

        P.analyze()
        dkeys = sorted(P.dma_final.keys(), key=str)
        esem = {e: es.enter_context(nc.semaphore("e_" + e)) for e in ("pe", "act", "dve", "pool", "sp")}
        dsem = {k: es.enter_context(nc.semaphore("d_%s_%s" % (k[0], k[1]))) for k in dkeys}
        for v in list(P.eng_final.values()) + list(P.dma_final.values()):
            assert v < 60000, v
        block = es.enter_context(nc.Block())
        per_eng = {}
        for o in P.ops:
            per_eng.setdefault(o.eng, []).append(o)

        def run_engine(eng_name, e):
            waited = {}
            for o in per_eng.get(eng_name, []):
                for d in o.deps:
                    if d.dma is not None:
                        key, val, sem = ("d", d.dma), d.dmaval, dsem[d.dma]
                    else:
                        key, val, sem = ("e", d.eng), d.sigval, esem[d.eng]
                    if waited.get(key, 0) >= val:
                        continue
                    waited[key] = val
                    e.wait_ge(sem, val)
                ins = o.fn(e)
                if o.dma is not None:
                    ins.then_inc(dsem[o.dma], 16)
                elif o.sig:
                    ins.then_inc(esem[o.eng], 1)
            if eng_name == "sp":
                for k in dkeys:
                    if k[0] == "s":
                        e.wait_ge(dsem[k], P.dma_final[k])

        @block.tensor
        def _(e):
            run_engine("pe", e)

        @block.scalar
        def _(e):
            run_engine("act", e)

        @block.vector
        def _(e):
            run_engine("dve", e)

        @block.gpsimd
        def _(e):
            run_engine("pool", e)

        @block.sync
        def _(e):
            run_engine("sp", e)
    return nc


_NC_CACHE = {}


def _consts(half):
    inv_freq = (10000.0 ** (-np.arange(0, 64, 2, dtype=np.float32) / 64)).astype(np.float32)
    cos_t = np.zeros((2, 128, WB), np.float32)
    sin_t = np.zeros((2, 128, WB), np.float32)
    p = np.arange(128)
    d = p % 64
    f = d % 32
    sign = np.where(d < 32, -1.0, 1.0).astype(np.float32)
    for ps_ in range(2):
        pos = np.zeros(WB, np.int32)
        pos[:768] = half * 1024 - 512 + ps_ * 768 + np.arange(768)
        if ps_ == 1:
            pos[768:] = 16384 + (np.arange(16) % 4)
        ang = pos.astype(np.float32)[:, None] * inv_freq[None, :]
        c = np.cos(ang).astype(np.float32)
        s = np.sin(ang).astype(np.float32)
        cos_t[ps_] = c[:, f].T
        sin_t[ps_] = s[:, f].T * sign[:, None]
    i = np.arange(128)[:, None]
    j = np.arange(128)[None, :]
    mask = np.full((128, 4, 256), NEG, np.float32)
    prev = np.where(j >= i, 0.0, NEG)
    cur = np.where(j <= i, 0.0, NEG)
    mask[:, 0, :128] = prev
    mask[:, 0, 128:] = cur
    mask[:, 1, :128] = prev if half == 1 else NEG
    mask[:, 1, 128:] = cur
    mask[:, 2, :128] = prev
    mask[:, 2, 128:132] = np.where(np.arange(4)[None, :] <= i, 0.0, NEG)
    ident = np.eye(128, dtype=np.float32)
    perm = np.zeros((128, 128), np.float32)
    for m in range(128):
        k = m + 32 if (m % 64) < 32 else m - 32
        perm[k, m] = 1.0
    flag = np.full((128, 1), 1.0 if half == 1 else 0.0, np.float32)
    return cos_t, sin_t, mask, ident, perm, flag


def kernel(x_prompt, x_sample, cache_k_win, cache_v_win, state_conv, ln_g, ln_b, w_in, sinks,
           conv_w, w_branch_attn, w_branch_conv, w_out, ffn1_gu, ffn1_down, ffn2_gu, ffn2_down):
    f32 = np.float32
    A = lambda a: np.ascontiguousarray(np.asarray(a, dtype=f32))
    x_prompt, x_sample = A(x_prompt), A(x_sample)
    cache_k_win, cache_v_win, state_conv = A(cache_k_win), A(cache_v_win), A(state_conv)
    shared = {
        "lng": A(A(ln_g).reshape(DEPTH, 3, 16, 128).transpose(3, 0, 1, 2).reshape(128, DEPTH * 48)),
        "lnb": A(A(ln_b).reshape(DEPTH, 3, 16, 128).transpose(3, 0, 1, 2).reshape(128, DEPTH * 48)),
        "w_in": A(w_in),
        "sinks": A(np.broadcast_to(A(sinks).reshape(1, DEPTH * 16), (128, DEPTH * 16))),
        "convw": A(A(conv_w).reshape(DEPTH, 3, 8, 128).transpose(3, 0, 1, 2).reshape(128, DEPTH * 24)),
        "wba": A(w_branch_attn), "wbc": A(w_branch_conv), "wout": A(w_out),
        "f1gu": A(ffn1_gu), "f1d": A(ffn1_down), "f2gu": A(ffn2_gu), "f2d": A(ffn2_down),
    }
    in_maps = []
    for c in range(8):
        b, half = c // 2, c % 2
        ext = np.zeros((1536, DM), f32)
        if half == 1:
            ext[:] = x_prompt[b, 512:2048]
        else:
            ext[512:] = x_prompt[b, 0:1024]
        xin = np.zeros((2, 128, KC, WB), f32)
        for ps_ in range(2):
            cols = np.zeros((WB, DM), f32)
            cols[:768] = ext[ps_ * 768:(ps_ + 1) * 768]
            if ps_ == 1:
                cols[768:] = x_sample[4 * c:4 * c + 4].reshape(16, DM)
            xin[ps_] = cols.T.reshape(KC, 128, WB).transpose(1, 0, 2)
        cos_t, sin_t, mask, ident, perm, flag = _consts(half)
        m = dict(shared)
        m.update({
            "xin": xin,
            "ck": A(cache_k_win[:, 4 * c:4 * c + 4].reshape(DEPTH, 4, 128, 256)),
            "cv": A(cache_v_win[:, 4 * c:4 * c + 4].reshape(DEPTH, 4, 128, 256)),
            "sc": A(state_conv[:, 4 * c:4 * c + 4]),
            "cst_cos": cos_t, "cst_sin": sin_t, "cst_mask": mask, "cst_ident": ident,
            "cst_perm": perm, "cst_flag": flag,
        })
        in_maps.append(m)
    if "nc" not in _NC_CACHE:
        _NC_CACHE["nc"] = build_program()
    nc = _NC_CACHE["nc"]
    ncr = int(os.environ.get("MK_CORES", "8"))
    res = run_bass_kernel_spmd(nc, in_maps[:ncr], core_ids=list(range(ncr)))
    R_ = res.results
    y_prompt = np.zeros((4, 2048, DM), f32)
    y_sample = np.zeros((32, 4, DM), f32)
    kwp = np.zeros((DEPTH, 4, 128, 4, 64), f32)
    vwp = np.zeros((DEPTH, 4, 128, 4, 64), f32)
    cvp = np.zeros((DEPTH, 4, 2, 1024), f32)
    kws = np.zeros((DEPTH, 32, 128, 4, 64), f32)
    vws = np.zeros((DEPTH, 32, 128, 4, 64), f32)
    cvs = np.zeros((DEPTH, 32, 2, 1024), f32)
    for c in range(ncr):
        b, half = c // 2, c % 2
        r = R_[c]
        y_prompt[b, half * 1024:(half + 1) * 1024] = np.asarray(r["y_own"])
        y_sample[4 * c:4 * c + 4] = np.asarray(r["y_smp"]).reshape(4, 4, DM)
        if half == 1:
            kwp[:, b] = np.asarray(r["kwin"]).reshape(DEPTH, 128, 4, 64)
            vwp[:, b] = np.asarray(r["vwin"]).reshape(DEPTH, 128, 4, 64)
            cvp[:, b] = np.asarray(r["convp"])
        kws[:, 4 * c:4 * c + 4] = np.asarray(r["ksmp"]).reshape(DEPTH, 4, 128, 4, 64)
        vws[:, 4 * c:4 * c + 4] = np.asarray(r["vsmp"]).reshape(DEPTH, 4, 128, 4, 64)
        cvs[:, 4 * c:4 * c + 4] = np.asarray(r["csmp"]).reshape(DEPTH, 4, 2, 1024)
    return (y_prompt, y_sample, kwp, vwp, cvp, kws, vws, cvs)
```

```python
import os
import numpy as np
import concourse.bass as bass
import concourse.mybir as mybir
from concourse.bass_utils import run_bass_kernel_spmd

F32 = mybir.dt.float32
BF16 = mybir.dt.bfloat16
AF = mybir.ActivationFunctionType
ALU = mybir.AluOpType
AX = mybir.AxisListType

DEPTH = int(os.environ.get("MK_DEPTH", "4"))
STAGE = int(os.environ.get("MK_STAGE", "99"))
DM = 2048
KC = 16
DFF = 5632
NIN = 8704
ALPHA = float(8.0 ** 0.25)
EPS = 1e-5
WB = 784
NSLOT = 4
NEG = -30000.0
ENABLE_MIX = True
SEGS = [(0, 12), (12, 24), (24, 36), (36, 44)]
OQ, OK_, OV, OCB, OCC, OCH, OGA, OGC = 0, 1024, 1280, 1536, 2560, 3584, 4608, 6656


class Op:
    __slots__ = ("eng", "fn", "reads", "writes", "dma", "idx", "sig", "sigval", "deps", "dmaval")


class Prog:
    def __init__(self):
        self.ops = []

    def op(self, eng, fn, reads=(), writes=(), dma=None):
        o = Op()
        o.eng, o.fn, o.reads, o.writes, o.dma = eng, fn, list(reads), list(writes), dma
        o.idx = len(self.ops)
        o.sig = False
        o.sigval = 0
        o.dmaval = 0
        o.deps = None
        self.ops.append(o)
        return o

    def analyze(self):
        state = {}
        last_dma = {}
        for o in self.ops:
            deps = {}

            def add(d, raw):
                if d is None or d is o:
                    return
                same = (d.dma is None and o.dma is None and d.eng == o.eng)
                if same and o.eng == "pe":
                    return
                deps[d.idx] = d

            for (buf, lo, hi) in o.reads:
                for g in range(lo, hi + 1):
                    st = state.get((buf, g))
                    if st is not None:
                        add(st[0], True)
                        if buf in ("psf", "psb"):
                            for r in st[1].values():
                                add(r, False)
            for (buf, lo, hi) in o.writes:
                for g in range(lo, hi + 1):
                    st = state.get((buf, g))
                    if st is not None:
                        add(st[0], False)
                        for r in st[1].values():
                            add(r, False)
                        for r in st[2]:
                            add(r, False)
            if o.dma is not None:
                p = last_dma.get(o.dma)
                if p is not None:
                    deps[p.idx] = p
                last_dma[o.dma] = o
            for (buf, lo, hi) in o.reads:
                for g in range(lo, hi + 1):
                    st = state.get((buf, g))
                    if st is None:
                        st = [None, {}, []]
                        state[(buf, g)] = st
                    if o.dma is None:
                        st[1][o.eng] = o
                    else:
                        st[2].append(o)
            for (buf, lo, hi) in o.writes:
                for g in range(lo, hi + 1):
                    state[(buf, g)] = [o, {}, []]
            best = {}
            out = []
            for d in deps.values():
                if d.dma is None:
                    b = best.get(d.eng)
                    if b is None or d.idx > b.idx:
                        best[d.eng] = d
                else:
                    out.append(d)
            for d in best.values():
                d.sig = True
                out.append(d)
            o.deps = out
        cnt = {}
        dcnt = {}
        for o in self.ops:
            if o.dma is not None:
                dcnt[o.dma] = dcnt.get(o.dma, 0) + 16
                o.dmaval = dcnt[o.dma]
            elif o.sig:
                cnt[o.eng] = cnt.get(o.eng, 0) + 1
                o.sigval = cnt[o.eng]
        self.dma_final = dcnt
        self.eng_final = cnt

    def emit(self, nc, block_engines, esem, dsem):
        per = {}
        for o in self.ops:
            per.setdefault(o.eng, []).append(o)
        for eng, ops in per.items():
            e = block_engines[eng]
            waited = {}
            for o in ops:
                for d in o.deps:
                    if d.dma is not None:
                        key, val = ("d", d.dma), d.dmaval
                        sem = dsem[d.dma]
                    else:
                        key, val = ("e", d.eng), d.sigval
                        sem = esem[d.eng]
                    if waited.get(key, 0) >= val:
                        continue
                    waited[key] = val
                    e.wait_ge(sem, val)
                ins = o.fn(e)
                if o.dma is not None:
                    ins.then_inc(dsem[o.dma], 16)
                elif o.sig:
                    ins.then_inc(esem[o.eng], 1)


class Buf:
    def __init__(self, name, ap3, es, n1, n2, gran=256):
        self.name, self.ap, self.es, self.n1, self.n2, self.gran = name, ap3, es, n1, n2, gran

    def reg(self, i0, i1, a, b):
        lo = (i0 * self.n2 + a) * self.es
        hi = ((i1 - 1) * self.n2 + b) * self.es - 1
        return (self.name, lo // self.gran, hi // self.gran)

    def s(self, i, a, b, p0=0, p1=128):
        return self.ap[p0:p1, i, a:b], self.reg(i, i + 1, a, b)

    def m(self, i0, i1, a, b, p0=0, p1=128):
        return self.ap[p0:p1, i0:i1, a:b], self.reg(i0, i1, a, b)


def col_tiles(c0, c1, samp):
    if samp:
        c1 = 784
    t = []
    a = c0
    while a < c1:
        b = min(a + 512, c1)
        t.append((a, b))
        a = b
    return t


def build_program():
    nc = bass.Bass("TRN2", target_bir_lowering=False)
    P = Prog()

    def din(name, shape, dt=F32):
        return nc.dram_tensor(name, list(shape), dt, kind="ExternalInput").ap()

    def dout(name, shape, dt=F32):
        return nc.dram_tensor(name, list(shape), dt, kind="ExternalOutput").ap()

    xin = din("xin", [2, 128, KC, WB])
    ck = din("ck", [DEPTH, 4, 128, 256])
    cv = din("cv", [DEPTH, 4, 128, 256])
    sc = din("sc", [DEPTH, 4, 2, 1024])
    lng = din("lng", [128, DEPTH * 3 * KC])
    lnb = din("lnb", [128, DEPTH * 3 * KC])
    w_in = din("w_in", [DEPTH, DM, NIN])
    sinks = din("sinks", [128, DEPTH * 16])
    convw = din("convw", [128, DEPTH * 3 * 8])
    wba = din("wba", [DEPTH, 1024, DM])
    wbc = din("wbc", [DEPTH, 1024, DM])
    wout = din("wout", [DEPTH, DM, DM])
    f1gu = din("f1gu", [DEPTH, DM, 2 * DFF])
    f1d = din("f1d", [DEPTH, DFF, DM])
    f2gu = din("f2gu", [DEPTH, DM, 2 * DFF])
    f2d = din("f2d", [DEPTH, DFF, DM])
    cst_cos = din("cst_cos", [2, 128, WB])
    cst_sin = din("cst_sin", [2, 128, WB])
    cst_mask = din("cst_mask", [128, 4, 256])
    cst_ident = din("cst_ident", [128, 128])
    cst_perm = din("cst_perm", [128, 128])
    cst_flag = din("cst_flag", [128, 1])

    y_own = dout("y_own", [1024, DM])
    y_smp = dout("y_smp", [16, DM])
    kwin = dout("kwin", [DEPTH, 128, 256])
    vwin = dout("vwin", [DEPTH, 128, 256])
    convp = dout("convp", [DEPTH, 2, 1024])
    ksmp = dout("ksmp", [DEPTH, 4, 128, 256])
    vsmp = dout("vsmp", [DEPTH, 4, 128, 256])
    csmp = dout("csmp", [DEPTH, 8, 1024])

    import contextlib
    es = contextlib.ExitStack()
    with es:
        def sb(name, shape, dt):
            return es.enter_context(nc.sbuf_tensor("s_" + name, list(shape), dt))

        R_t = sb("R", [128, KC, WB], F32)
        xb_t = sb("xb", [128, KC, WB], BF16)
        wr_t = sb("wring", [128, NSLOT, 16, 256], BF16)
        NBF = 27008
        NF = 5400
        abf_t = sb("abf", [128, NBF], BF16)
        af_t = sb("af", [128, NF], F32)
        cos_t = sb("cos", [128, 1, WB], F32)
        sin_t = sb("sin", [128, 1, WB], F32)
        mask_t = sb("mask", [128, 4, 256], F32)
        idf_t = sb("idf", [128, 1, 128], F32)
        idb_t = sb("idb", [128, 1, 128], BF16)
        perm_t = sb("perm", [128, 1, 128], F32)
        ones_t = sb("ones", [128, 1, 128], BF16)
        flag_t = sb("flag", [128, 1, 1], F32)
        lng_t = sb("lng", [128, 1, DEPTH * 3 * KC], F32)
        lnb_t = sb("lnb", [128, 1, DEPTH * 3 * KC], F32)
        lnga_t = sb("lnga", [128, 1, DEPTH * 3 * KC], F32)
        lnba_t = sb("lnba", [128, 1, DEPTH * 3 * KC], F32)
        snk_t = sb("snk", [128, 1, DEPTH * 16], F32)
        nsnk_t = sb("nsnk", [128, 1, DEPTH * 16], F32)
        cw_t = sb("cw", [128, 1, DEPTH * 24], F32)
        kst_t = sb("kst", [128, DEPTH * 4, 128], BF16)
        vst_t = sb("vst", [128, DEPTH, 256], BF16)
        ust_t = sb("ust", [128, DEPTH * 8, 2], F32)
        st_t = sb("stat", [128, 4, 64], F32)
        osb_t = sb("osb", [128, 2, 512], F32)
        uo_t = sb("uo", [128, 8, 10], F32)
        psf_t = es.enter_context(nc.psum_tensor("psf", [128, 6, 512], F32))
        psb_t = es.enter_context(nc.psum_tensor("psb", [128, 2, 1024], BF16))

        R = Buf("R", R_t, 4, KC, WB)
        XB = Buf("xb", xb_t, 2, KC, WB)
        COS = Buf("cos", cos_t, 4, 1, WB)
        SIN = Buf("sin", sin_t, 4, 1, WB)
        MASK = Buf("mask", mask_t, 4, 4, 256)
        IDF = Buf("idf", idf_t, 4, 1, 128)
        IDB = Buf("idb", idb_t, 2, 1, 128)
        PERM = Buf("perm", perm_t, 4, 1, 128)
        ONES = Buf("ones", ones_t, 2, 1, 128)
        FLAG = Buf("flag", flag_t, 4, 1, 1)
        LNG = Buf("lng", lng_t, 4, 1, 192)
        LNB = Buf("lnb", lnb_t, 4, 1, 192)
        LNGA = Buf("lnga", lnga_t, 4, 1, 192)
        LNBA = Buf("lnba", lnba_t, 4, 1, 192)
        SNK = Buf("snk", snk_t, 4, 1, 64)
        NSNK = Buf("nsnk", nsnk_t, 4, 1, 64)
        CW = Buf("cw", cw_t, 4, 1, 96)
        KST = Buf("kst", kst_t, 2, DEPTH * 4, 128)
        VST = Buf("vst", vst_t, 2, DEPTH, 256)
        UST = Buf("ust", ust_t, 4, DEPTH * 8, 2, gran=8)
        STAT = Buf("stat", st_t, 4, 4, 64, gran=16)
        OSB = Buf("osb", osb_t, 4, 2, 512)
        UO = Buf("uo", uo_t, 4, 8, 10, gran=8)
        PSF = Buf("psf", psf_t, 4, 6, 512, gran=2048)
        PSB = Buf("psb", psb_t, 2, 2, 1024, gran=2048)

        def abf(off, n1, n2):
            assert off + n1 * n2 <= NBF, (off, n1, n2)
            ap = abf_t[:, off:off + n1 * n2].rearrange("p (a b) -> p a b", b=n2)
            b = Buf("abf", ap, 2, n1, n2)
            base = off
            oreg = b.reg
            b.reg = lambda i0, i1, a, bb, _o=oreg, _b=base: (
                "abf", ((_b + i0 * n2 + a) * 2) // 256, ((_b + (i1 - 1) * n2 + bb) * 2 - 1) // 256)
            return b

        def af(off, n1, n2):
            assert off + n1 * n2 <= NF, (off, n1, n2)
            ap = af_t[:, off:off + n1 * n2].rearrange("p (a b) -> p a b", b=n2)
            b = Buf("af", ap, 4, n1, n2)
            base = off
            b.reg = lambda i0, i1, a, bb, _b=base: (
                "af", ((_b + i0 * n2 + a) * 4) // 256, ((_b + (i1 - 1) * n2 + bb) * 4 - 1) // 256)
            return b

        H = abf(0, 12, WB)
        YBF = abf(9408, 2, 512)
        YSQ = abf(10432, 2, 512)
        SG = af(0, 2, 512)
        MEAN = af(1024, 2, 512)
        RSTD = af(2048, 2, 512)
        VTMP = af(3072, 2, 512)
        PA = abf(0, 16, WB)
        QB = abf(0, 8, WB)
        KD = abf(6272, 4, WB)
        VB = abf(9408, 7, 256)
        VS = abf(11200, 4, 256)
        ATT = abf(12544, 8, WB)
        PB_ = abf(18816, 2, 1024)
        PT = abf(20864, 2, 1024)
        KCT = abf(22912, 16, 128)
        ATOK = abf(24960, 1, 1024)
        VCB = abf(25984, 4, 256)
        YC = abf(18816, 8, WB)
        KF = af(0, 2, 512)
        RT = af(1024, 2, 512)
        SM = af(2048, 2, 1024)
        K32 = af(4096, 4, 144)
        CKD = af(4672, 4, 128)
        UU = af(0, 2, 800)
        CBF = af(1600, 1, WB)
        CCF = af(2384, 1, WB)
        SG2 = af(3168, 2, 512)
        TM2 = af(4192, 2, 512)

        rr = {"w": 0, "pf": 0, "pb": 0, "sg": 0, "st": 0, "o": 0, "pp": 0, "uu": 0, "s4": 0}

        def nxt(k, n):
            v = rr[k]
            rr[k] = (v + 1) % n
            return v

        def wtile(src_ap, nk, ncols):
            s = nxt("w", NSLOT)
            reg = ("wr", s, s)
            dst = wr_t[:, s, 0:nk, 0:ncols]
            P.op("pool", lambda e, d=dst, a=src_ap: e.dma_start(out=d, in_=a), writes=[reg], dma=("w", s))
            return s, reg

        def wv(dram3, l, r0, nk, c0, ncols):
            return dram3[l, r0:r0 + nk * 128, c0:c0 + ncols].rearrange("(k p) n -> p k n", p=128)

        def mm(out, lhsT, rhs, start, stop, reads, writes):
            P.op("pe", lambda e: e.matmul(out, lhsT=lhsT, rhs=rhs, start=start, stop=stop), reads=reads, writes=writes)

        def psf():
            return nxt("pf", 6)

        def ld(eng, dst_ap, dst_reg, src_ap, key):
            P.op(eng, lambda e: e.dma_start(out=dst_ap, in_=src_ap), writes=[dst_reg], dma=key)

        ld("sp", mask_t[:], MASK.reg(0, 4, 0, 256), cst_mask, ("c", 0))
        ld("sp", idf_t[:, 0, :], IDF.reg(0, 1, 0, 128), cst_ident, ("c", 1))
        ld("sp", perm_t[:, 0, :], PERM.reg(0, 1, 0, 128), cst_perm, ("c", 2))
        ld("sp", flag_t[:, 0, :], FLAG.reg(0, 1, 0, 1), cst_flag, ("c", 3))
        ld("sp", lng_t[:, 0, :], LNG.reg(0, 1, 0, 192), lng, ("c", 0))
        ld("sp", lnb_t[:, 0, :], LNB.reg(0, 1, 0, 192), lnb, ("c", 1))
        ld("sp", snk_t[:, 0, :], SNK.reg(0, 1, 0, 64), sinks, ("c", 2))
        ld("sp", cw_t[:, 0, :], CW.reg(0, 1, 0, 96), convw, ("c", 3))
        P.op("dve", lambda e: e.tensor_copy(out=idb_t[:, 0, :], in_=idf_t[:, 0, :]),
             reads=[IDF.reg(0, 1, 0, 128)], writes=[IDB.reg(0, 1, 0, 128)])
        P.op("dve", lambda e: e.memset(ones_t[:, 0, :], 1.0 / DM), writes=[ONES.reg(0, 1, 0, 128)])
        P.op("dve", lambda e: e.tensor_scalar_mul(out=lnga_t[:, 0, :], in0=lng_t[:, 0, :], scalar1=ALPHA),
             reads=[LNG.reg(0, 1, 0, 192)], writes=[LNGA.reg(0, 1, 0, 192)])
        P.op("dve", lambda e: e.tensor_scalar_mul(out=lnba_t[:, 0, :], in0=lnb_t[:, 0, :], scalar1=ALPHA),
             reads=[LNB.reg(0, 1, 0, 192)], writes=[LNBA.reg(0, 1, 0, 192)])
        P.op("dve", lambda e: e.tensor_scalar_mul(out=nsnk_t[:, 0, :], in0=snk_t[:, 0, :], scalar1=-1.0),
             reads=[SNK.reg(0, 1, 0, 64)], writes=[NSNK.reg(0, 1, 0, 64)])

        def ffn(l, gu, dn, tiles):
            for (c_lo, c_hi) in SEGS:
                nch = c_hi - c_lo
                for ci in range(c_lo, c_hi, 2):
                    sg_, rg = wtile(wv(gu, l, 0, 16, ci * 128, 256), 16, 256)
                    su_, ru = wtile(wv(gu, l, 0, 16, DFF + ci * 128, 256), 16, 256)
                    for (a, b) in tiles:
                        n = b - a
                        for hf in range(2):
                            bg, bu = psf(), psf()
                            pg, pgr = PSF.s(bg, 0, n)
                            pu, pur = PSF.s(bu, 0, n)
                            for k in range(KC):
                                x_ap, x_r = XB.s(k, a, b)
                                mm(pg, wr_t[:, sg_, k, hf * 128:(hf + 1) * 128], x_ap, k == 0, k == KC - 1, [rg, x_r], [pgr])
                            for k in range(KC):
                                x_ap, x_r = XB.s(k, a, b)
                                mm(pu, wr_t[:, su_, k, hf * 128:(hf + 1) * 128], x_ap, k == 0, k == KC - 1, [ru, x_r], [pur])
                            si = nxt("sg", 2)
                            s_ap, s_r = SG.s(si, 0, n)
                            P.op("act", lambda e, o=s_ap, i=pg: e.activation(out=o, in_=i, func=AF.Silu), reads=[pgr], writes=[s_r])
                            h_ap, h_r = H.s(ci - c_lo + hf, a, b)
                            P.op("dve", lambda e, o=h_ap, i0=s_ap, i1=pu: e.tensor_tensor(out=o, in0=i0, in1=i1, op=ALU.mult),
                                 reads=[s_r, pur], writes=[h_r])
                for j in range(0, KC, 2):
                    sd, rd = wtile(wv(dn, l, c_lo * 128, nch, j * 128, 256), nch, 256)
                    for (a, b) in tiles:
                        n = b - a
                        for hf in range(2):
                            bo = psf()
                            po, por = PSF.s(bo, 0, n)
                            for k in range(nch):
                                h_ap, h_r = H.s(k, a, b)
                                mm(po, wr_t[:, sd, k, hf * 128:(hf + 1) * 128], h_ap, k == 0, k == nch - 1, [rd, h_r], [por])
                            r_ap, r_r = R.s(j + hf, a, b)
                            P.op("dve", lambda e, o=r_ap, i=po: e.scalar_tensor_tensor(out=o, in0=i, scalar=0.5, in1=o, op0=ALU.mult, op1=ALU.add),
                                 reads=[por, r_r], writes=[r_r])

        def layernorm(l, which, tiles, final=False):
            pbase = (l * 3 + which) * KC
            for (a, b) in tiles:
                n = b - a
                bm, bq = psf(), psf()
                pm, pmr = PSF.s(bm, 0, n)
                pq, pqr = PSF.s(bq, 0, n)
                for k in range(KC):
                    r_ap, r_r = R.s(k, a, b)
                    i1 = nxt("st", 2)
                    yb, ybr = YBF.s(i1, 0, n)
                    ys, ysr = YSQ.s(i1, 0, n)
                    P.op("dve", lambda e, o=yb, i=r_ap: e.tensor_copy(out=o, in_=i), reads=[r_r], writes=[ybr])
                    P.op("act", lambda e, o=ys, i=r_ap: e.activation(out=o, in_=i, func=AF.Square), reads=[r_r], writes=[ysr])
                    mm(pm, ones_t[:, 0, :], yb, k == 0, k == KC - 1, [ONES.reg(0, 1, 0, 128), ybr], [pmr])
                    mm(pq, ones_t[:, 0, :], ys, k == 0, k == KC - 1, [ONES.reg(0, 1, 0, 128), ysr], [pqr])
                si = nxt("sg", 2)
                mn, mnr = MEAN.s(si, 0, n)
                rs, rsr = RSTD.s(si, 0, n)
                vt, vtr = VTMP.s(si, 0, n)
                P.op("act", lambda e, o=mn, i=pm: e.copy(out=o, in_=i), reads=[pmr], writes=[mnr])
                P.op("dve", lambda e, o=vt, i=mn: e.tensor_tensor(out=o, in0=i, in1=i, op=ALU.mult), reads=[mnr], writes=[vtr])
                P.op("dve", lambda e, o=vt, i=pq: e.tensor_tensor(out=o, in0=i, in1=o, op=ALU.subtract), reads=[pqr, vtr], writes=[vtr])
                P.op("dve", lambda e, o=vt: e.tensor_scalar_add(out=o, in0=o, scalar1=EPS), reads=[vtr], writes=[vtr])
                P.op("act", lambda e, o=vt: e.activation(out=o, in_=o, func=AF.Sqrt), reads=[vtr], writes=[vtr])
                P.op("dve", lambda e, o=rs, i=vt: e.reciprocal(out=o, in_=i), reads=[vtr], writes=[rsr])
                for k in range(KC):
                    r_ap, r_r = R.s(k, a, b)
                    x_ap, x_r = XB.s(k, a, b)
                    P.op("dve", lambda e, o=r_ap, i=mn: e.tensor_tensor(out=o, in0=o, in1=i, op=ALU.subtract), reads=[r_r, mnr], writes=[r_r])
                    P.op("dve", lambda e, o=r_ap, i=rs: e.tensor_tensor(out=o, in0=o, in1=i, op=ALU.mult), reads=[r_r, rsr], writes=[r_r])
                    g1 = lng_t[:, 0, pbase + k:pbase + k + 1]
                    b1 = lnb_t[:, 0, pbase + k:pbase + k + 1]
                    P.op("act", lambda e, o=x_ap, i=r_ap, g=g1, bb=b1: e.activation(out=o, in_=i, func=AF.Identity, bias=bb, scale=g),
                         reads=[r_r, LNG.reg(0, 1, 0, 192), LNB.reg(0, 1, 0, 192)], writes=[x_r])
                    if final:
                        g2, b2 = g1, b1
                        rg_, rb_ = LNG.reg(0, 1, 0, 192), LNB.reg(0, 1, 0, 192)
                    else:
                        g2 = lnga_t[:, 0, pbase + k:pbase + k + 1]
                        b2 = lnba_t[:, 0, pbase + k:pbase + k + 1]
                        rg_, rb_ = LNGA.reg(0, 1, 0, 192), LNBA.reg(0, 1, 0, 192)
                    P.op("act", lambda e, o=r_ap, g=g2, bb=b2: e.activation(out=o, in_=o, func=AF.Identity, bias=bb, scale=g),
                         reads=[r_r, rg_, rb_], writes=[r_r])

        def rope(ps_ap, ps_reg, a, b, out_ap, out_reg, k32=()):
            n = b - a
            si = nxt("sg", 2)
            kf, kfr = KF.s(si, 0, n)
            rt, rtr = RT.s(si, 0, n)
            P.op("act", lambda e: e.copy(out=kf, in_=ps_ap), reads=[ps_reg], writes=[kfr])
            br = psf()
            pr, prr = PSF.s(br, 0, n)
            mm(pr, perm_t[:, 0, :], kf, True, True, [PERM.reg(0, 1, 0, 128), kfr], [prr])
            c_ap, c_r = COS.s(0, a, b)
            s_ap, s_r = SIN.s(0, a, b)
            P.op("dve", lambda e: e.tensor_tensor(out=rt, in0=kf, in1=c_ap, op=ALU.mult), reads=[kfr, c_r], writes=[rtr])
            P.op("dve", lambda e: e.tensor_tensor(out=kf, in0=pr, in1=s_ap, op=ALU.mult), reads=[prr, s_r, kfr], writes=[kfr])
            P.op("dve", lambda e: e.tensor_tensor(out=out_ap, in0=rt, in1=kf, op=ALU.add), reads=[rtr, kfr], writes=[out_reg])
            for (o32, o32r, ca, cb) in k32:
                P.op("dve", lambda e, o=o32, x=RT.ap[:, si, ca:cb], y=KF.ap[:, si, ca:cb]: e.tensor_tensor(out=o, in0=x, in1=y, op=ALU.add),
                     reads=[rtr, kfr], writes=[o32r])

        def attn_unit(l, nq, qa, chunk, g, segs, mask_ap, Wt, uidx):
            bSs = [psf(), psf()]
            st = nxt("s4", 4)
            pp = nxt("pp", 2)
            sregs = [PSF.reg(bSs[0], bSs[0] + 1, 0, 512), PSF.reg(bSs[1], bSs[1] + 1, 0, 512)]
            streg = STAT.reg(st, st + 1, 0, 64)
            qreg = QB.reg(chunk, chunk + 1, qa, qa + nq)
            nseg = len(segs)
            for hh in range(2):
                pb = hh * 64
                off = 0
                for (kfn, v_ap, v_r, w) in segs:
                    k_ap, k_r = kfn(pb)
                    mm(psf_t[0:nq, bSs[hh], off:off + w], QB.ap[pb:pb + 64, chunk, qa:qa + nq], k_ap,
                       True, True, [qreg, k_r], [sregs[hh]])
                    off += w
            smr = SM.reg(pp, pp + 1, 0, 2 * Wt)
            for hh in range(2):
                P.op("dve", lambda e, hh=hh: e.tensor_tensor(out=SM.ap[0:nq, pp, hh * Wt:(hh + 1) * Wt],
                                                            in0=psf_t[0:nq, bSs[hh], 0:Wt], in1=mask_ap, op=ALU.add),
                     reads=[sregs[hh], MASK.reg(0, 4, 0, 256)], writes=[smr])
            sm3 = SM.ap[0:nq, pp, 0:2 * Wt].rearrange("p (h w) -> p h w", w=Wt)
            P.op("dve", lambda e: e.tensor_reduce(out=st_t[0:nq, st, 0:2], in_=sm3, axis=AX.X, op=ALU.max), reads=[smr], writes=[streg])
            P.op("dve", lambda e: e.memset(st_t[0:nq, st, 4:6], 0.0), writes=[streg])
            for hh in range(2):
                h = 2 * chunk + hh
                P.op("dve", lambda e, hh=hh, h=h: e.tensor_scalar(out=st_t[0:nq, st, 2 + hh:3 + hh], in0=st_t[0:nq, st, hh:hh + 1],
                                                                  scalar1=-0.125, scalar2=nsnk_t[0:nq, 0, l * 16 + h:l * 16 + h + 1],
                                                                  op0=ALU.mult, op1=ALU.min),
                     reads=[streg, NSNK.reg(0, 1, 0, 64)], writes=[streg])
            yield
            pbr = PB_.reg(pp, pp + 1, 0, 2 * Wt)
            for hh in range(2):
                h = 2 * chunk + hh
                P.op("act", lambda e, hh=hh: e.activation(out=PB_.ap[0:nq, pp, hh * Wt:(hh + 1) * Wt], in_=SM.ap[0:nq, pp, hh * Wt:(hh + 1) * Wt],
                                                          func=AF.Exp, bias=st_t[0:nq, st, 2 + hh:3 + hh], scale=0.125,
                                                          accum_out=st_t[0:nq, st, 4 + hh:5 + hh]),
                     reads=[smr, streg], writes=[pbr, streg])
                P.op("act", lambda e, hh=hh, h=h: e.activation(out=st_t[0:nq, st, 6 + hh:7 + hh], in_=snk_t[0:nq, 0, l * 16 + h:l * 16 + h + 1],
                                                               func=AF.Exp, bias=st_t[0:nq, st, 2 + hh:3 + hh], scale=1.0),
                     reads=[streg, SNK.reg(0, 1, 0, 64)], writes=[streg])
            P.op("dve", lambda e: e.tensor_tensor(out=st_t[0:nq, st, 8:10], in0=st_t[0:nq, st, 4:6], in1=st_t[0:nq, st, 6:8], op=ALU.add),
                 reads=[streg], writes=[streg])
            P.op("dve", lambda e: e.reciprocal(out=st_t[0:nq, st, 10:12], in_=st_t[0:nq, st, 8:10]), reads=[streg], writes=[streg])
            bT = uidx % 2
            treg = PSB.reg(bT, bT + 1, 0, 1024)
            for hh in range(2):
                off = 0
                for si, (kfn, v_ap, v_r, w) in enumerate(segs):
                    P.op("pe", lambda e, hh=hh, si=si, off=off, w=w: e.transpose(
                        psb_t[0:w, bT, (hh * nseg + si) * 128:(hh * nseg + si) * 128 + nq],
                        PB_.ap[0:nq, pp, hh * Wt + off:hh * Wt + off + w], idb_t[0:nq, 0, 0:nq]),
                        reads=[pbr, IDB.reg(0, 1, 0, 128)], writes=[treg])
                    off += w
            yield
            ptreg = PT.reg(pp, pp + 1, 0, 512)
            ncol = nseg * 2 * 128
            P.op("act", lambda e: e.copy(out=PT.ap[:, pp, 0:ncol], in_=psb_t[:, bT, 0:ncol]), reads=[treg], writes=[ptreg])
            bO = psf()
            oreg = PSF.reg(bO, bO + 1, 0, 512)
            for hh in range(2):
                for si, (kfn, v_ap, v_r, w) in enumerate(segs):
                    mm(psf_t[0:nq, bO, hh * 64:(hh + 1) * 64], PT.ap[0:w, pp, (hh * nseg + si) * 128:(hh * nseg + si) * 128 + nq], v_ap,
                       si == 0, si == nseg - 1, [ptreg, v_r], [oreg])
            yield
            for hh in range(2):
                h = 2 * chunk + hh
                P.op("act", lambda e, hh=hh, h=h: e.activation(out=ATOK.ap[0:nq, 0, h * 64:(h + 1) * 64], in_=psf_t[0:nq, bO, hh * 64:(hh + 1) * 64],
                                                               func=AF.Identity, scale=st_t[0:nq, st, 10 + hh:11 + hh]),
                     reads=[oreg, streg], writes=[ATOK.reg(0, 1, h * 64, (h + 1) * 64)])

        def run_units(units):
            pending = list(units)
            live = []
            while pending or live:
                nl = []
                if pending:
                    g_, fin = pending.pop(0)
                    next(g_)
                    nl.append((g_, fin))
                for (g_, fin) in live:
                    try:
                        next(g_)
                        nl.append((g_, fin))
                    except StopIteration:
                        if fin is not None:
                            fin()
                live = nl

        def attn_finish(nq, qa, bT):
            treg = PSB.reg(bT, bT + 1, 0, 1024)
            for c in range(8):
                P.op("pe", lambda e, c=c: e.transpose(psb_t[:, bT, c * 128:c * 128 + nq], ATOK.ap[0:nq, 0, c * 128:(c + 1) * 128], idb_t[0:nq, 0, 0:nq]),
                     reads=[ATOK.reg(0, 1, c * 128, (c + 1) * 128), IDB.reg(0, 1, 0, 128)], writes=[treg])
            src = psb_t[:, bT, :].rearrange("p (c n) -> p c n", n=128)[:, :, 0:nq]
            P.op("dve", lambda e: e.tensor_copy(out=ATT.ap[:, 0:8, qa:qa + nq], in_=src), reads=[treg], writes=[ATT.reg(0, 8, qa, qa + nq)])

        def MIX(l, ps_, c0, tiles, tiles_r):
            samp = (ps_ == 1)
            first = c0 // 128
            blocks = list(range(first, 6))
            for t2 in range(2):
                s_ = nxt("w", NSLOT)
                reg = ("wr", s_, s_)
                for g2 in range(2):
                    for dup in range(2):
                        col = OK_ + (2 * t2 + g2) * 64
                        src = w_in[l, :, col:col + 64].rearrange("(k p) n -> p k n", p=128)
                        dst = wr_t[:, s_, :, g2 * 128 + dup * 64:g2 * 128 + dup * 64 + 64]
                        P.op("pool", lambda e, d=dst, a=src: e.dma_start(out=d, in_=a), writes=[reg], dma=("w", s_))
                for (a, b) in tiles:
                    n = b - a
                    for g2 in range(2):
                        g = 2 * t2 + g2
                        bk = psf()
                        pk, pkr = PSF.s(bk, 0, n)
                        for k in range(KC):
                            x_ap, x_r = XB.s(k, a, b)
                            mm(pk, wr_t[:, s_, k, g2 * 128:(g2 + 1) * 128], x_ap, k == 0, k == KC - 1, [reg, x_r], [pkr])
                        o_ap, o_r = KD.s(g, a, b)
                        k32 = []
                        if samp:
                            for (r_lo, r_hi, d_off) in ((640, 768, 0), (768, 784, 128)):
                                lo, hi = max(a, r_lo), min(b, r_hi)
                                if lo < hi:
                                    d_ap, d_r = K32.s(g, d_off + lo - r_lo, d_off + hi - r_lo)
                                    k32.append((d_ap, d_r, lo - a, hi - a))
                        rope(pk, pkr, a, b, o_ap, o_r, k32)
            if not samp:
                for g in range(4):
                    s_ap, s_r = KD.s(g, 640, 768)
                    d_ap, d_r = KST.s(l * 4 + g, 0, 128)
                    P.op("dve", lambda e, o=d_ap, i=s_ap: e.tensor_copy(out=o, in_=i), reads=[s_r], writes=[d_r])
            else:
                bo = psf()
                for g in range(4):
                    i_ap, i_r = K32.s(g, 0, 128)
                    o_ap, o_r = PSF.s(bo, g * 128, (g + 1) * 128)
                    P.op("pe", lambda e, o=o_ap, i=i_ap: e.transpose(o, i, idf_t[:, 0, :]), reads=[i_r, IDF.reg(0, 1, 0, 128)], writes=[o_r])
                oi = nxt("o", 2)
                P.op("act", lambda e, oi=oi, bo=bo: e.copy(out=osb_t[:, oi, 0:256].rearrange("p (g e) -> p g e", e=64),
                                                       in_=psf_t[:, bo, :].rearrange("p (g e) -> p g e", e=128)[:, :, 0:64]),
                     reads=[PSF.reg(bo, bo + 1, 0, 512)], writes=[OSB.reg(oi, oi + 1, 0, 512)])
                store(kwin[l], osb_t[:, oi, 0:256], OSB.reg(oi, oi + 1, 0, 512))
                bo = psf()
                for g in range(4):
                    i_ap, i_r = K32.s(g, 128, 144)
                    P.op("pe", lambda e, g=g, i=i_ap, bo=bo: e.transpose(psf_t[0:16, bo, g * 128:(g + 1) * 128], i, idf_t[:, 0, :]),
                         reads=[i_r, IDF.reg(0, 1, 0, 128)], writes=[PSF.reg(bo, bo + 1, 0, 512)])
                oi = nxt("o", 2)
                P.op("act", lambda e, oi=oi, bo=bo: e.copy(out=osb_t[0:16, oi, 0:256].rearrange("p (g e) -> p g e", e=64),
                                                       in_=psf_t[0:16, bo, :].rearrange("p (g e) -> p g e", e=128)[:, :, 0:64]),
                     reads=[PSF.reg(bo, bo + 1, 0, 512)], writes=[OSB.reg(oi, oi + 1, 0, 512)])
                for s4 in range(4):
                    store(ksmp[l, s4, 124:128, :], osb_t[4 * s4:4 * s4 + 4, oi, 0:256], OSB.reg(oi, oi + 1, 0, 512))
            if STAGE < 4:
                return
            for t4 in range(4):
                sq, rq = wtile(wv(w_in, l, 0, 16, OQ + t4 * 256, 256), 16, 256)
                for (a, b) in tiles_r:
                    n = b - a
                    for hf in range(2):
                        bq_ = psf()
                        pq_, pqr_ = PSF.s(bq_, 0, n)
                        for k in range(KC):
                            x_ap, x_r = XB.s(k, a, b)
                            mm(pq_, wr_t[:, sq, k, hf * 128:(hf + 1) * 128], x_ap, k == 0, k == KC - 1, [rq, x_r], [pqr_])
                        o_ap, o_r = QB.s(2 * t4 + hf, a, b)
                        rope(pq_, pqr_, a, b, o_ap, o_r)
            if STAGE < 5:
                return
            sv, rv = wtile(wv(w_in, l, 0, 16, OV, 256), 16, 256)
            for nb_ in blocks:
                ca = nb_ * 128
                bv = psf()
                pv, pvr = PSF.s(bv, 0, 256)
                for k in range(KC):
                    mm(pv, XB.ap[:, k, ca:ca + 128], wr_t[:, sv, k, 0:256], k == 0, k == KC - 1, [XB.reg(k, k + 1, ca, ca + 128), rv], [pvr])
                v_ap, v_r = VB.s(nb_, 0, 256)
                P.op("act", lambda e, o=v_ap, i=pv: e.copy(out=o, in_=i), reads=[pvr], writes=[v_r])
                if nb_ == 5 and not samp:
                    d_ap, d_r = VST.s(l, 0, 256)
                    P.op("dve", lambda e, o=d_ap, i=v_ap: e.tensor_copy(out=o, in_=i), reads=[v_r], writes=[d_r])
                if nb_ == 5 and samp:
                    oi = nxt("o", 2)
                    P.op("dve", lambda e, oi=oi, i=pv: e.tensor_copy(out=osb_t[:, oi, 0:256], in_=i), reads=[pvr], writes=[OSB.reg(oi, oi + 1, 0, 512)])
                    store(vwin[l], osb_t[:, oi, 0:256], OSB.reg(oi, oi + 1, 0, 512))
            if samp:
                for s4 in range(4):
                    qa = 768 + 4 * s4
                    bv = psf()
                    pvr = PSF.reg(bv, bv + 1, 0, 512)
                    for k in range(KC):
                        mm(psf_t[0:4, bv, 0:256], XB.ap[:, k, qa:qa + 4], wr_t[:, sv, k, 0:256], k == 0, k == KC - 1,
                           [XB.reg(k, k + 1, qa, qa + 4), rv], [pvr])
                    P.op("act", lambda e, s4=s4, bv=bv: e.copy(out=VS.ap[0:4, s4, :], in_=psf_t[0:4, bv, 0:256]), reads=[pvr], writes=[VS.reg(s4, s4 + 1, 0, 256)])
                    oi = nxt("o", 2)
                    P.op("dve", lambda e, oi=oi, bv=bv: e.tensor_copy(out=osb_t[0:4, oi, 0:256], in_=psf_t[0:4, bv, 0:256]), reads=[pvr],
                         writes=[OSB.reg(oi, oi + 1, 0, 512)])
                    store(vsmp[l, s4, 124:128, :], osb_t[0:4, oi, 0:256], OSB.reg(oi, oi + 1, 0, 512))
                for s4 in range(4):
                    srck = ck[l, s4].rearrange("t (g e) -> t g e", e=64)
                    for dup in range(2):
                        ld("sp", CKD.ap[:, :, dup * 64:(dup + 1) * 64], CKD.reg(0, 4, 0, 128), srck, ("k", dup))
                    bo = psf()
                    for g in range(4):
                        i_ap, i_r = CKD.s(g, 0, 128)
                        o_ap, o_r = PSF.s(bo, g * 128, (g + 1) * 128)
                        P.op("pe", lambda e, o=o_ap, i=i_ap: e.transpose(o, i, idf_t[:, 0, :]), reads=[i_r, IDF.reg(0, 1, 0, 128)], writes=[o_r])
                    P.op("act", lambda e, s4=s4, bo=bo: e.copy(out=KCT.ap[:, s4 * 4:s4 * 4 + 4, :], in_=psf_t[:, bo, :].rearrange("p (g e) -> p g e", e=128)),
                         reads=[PSF.reg(bo, bo + 1, 0, 512)], writes=[KCT.reg(s4 * 4, s4 * 4 + 4, 0, 128)])
                    sm_ap, sm_r = SM.s(s4 % 2, 0, 256)
                    ld("sp", sm_ap, sm_r, cv[l, s4], ("k", 2))
                    vc_ap, vc_r = VCB.s(s4, 0, 256)
                    P.op("dve", lambda e, o=vc_ap, i=sm_ap: e.tensor_copy(out=o, in_=i), reads=[sm_r], writes=[vc_r])
                    P.op("sp", lambda e, s4=s4: e.dma_start(out=ksmp[l, s4, 0:124, :], in_=ck[l, s4, 4:128, :]), dma=("s", st_keys["n"] % 4))
                    st_keys["n"] += 1
                    P.op("sp", lambda e, s4=s4: e.dma_start(out=vsmp[l, s4, 0:124, :], in_=cv[l, s4, 4:128, :]), dma=("s", st_keys["n"] % 4))
                    st_keys["n"] += 1
            if STAGE < 6:
                return
            units = []
            for nb_ in (blocks if samp else blocks[1:]):
                ca = nb_ * 128
                for chunk in range(8):
                    g = chunk // 2
                    segs = []
                    if nb_ > first:
                        segs.append((lambda pb, g=g, ca=ca: (KD.ap[pb:pb + 64, g, ca - 128:ca], KD.reg(g, g + 1, ca - 128, ca)),
                                     VB.ap[:, nb_ - 1, g * 64:(g + 1) * 64], VB.reg(nb_ - 1, nb_, 0, 256), 128))
                    elif samp:
                        segs.append((lambda pb, g=g: (KST.ap[pb:pb + 64, l * 4 + g, 0:128], KST.reg(l * 4 + g, l * 4 + g + 1, 0, 128)),
                                     VST.ap[:, l, g * 64:(g + 1) * 64], VST.reg(l, l + 1, 0, 256), 128))
                    segs.append((lambda pb, g=g, ca=ca: (KD.ap[pb:pb + 64, g, ca:ca + 128], KD.reg(g, g + 1, ca, ca + 128)),
                                 VB.ap[:, nb_, g * 64:(g + 1) * 64], VB.reg(nb_, nb_ + 1, 0, 256), 128))
                    if len(segs) == 1:
                        m_ap, Wt = mask_t[:, 0, 128:256], 128
                    elif (not samp) and nb_ == 4:
                        m_ap, Wt = mask_t[:, 1, 0:256], 256
                    else:
                        m_ap, Wt = mask_t[:, 0, 0:256], 256
                    ux = len(units)
                    units.append((attn_unit(l, 128, ca, chunk, g, segs, m_ap, Wt, ux),
                                  (lambda ca=ca, ux=ux: attn_finish(128, ca, (ux + 1) % 2)) if chunk == 7 else None))
            if samp:
                for s4 in range(4):
                    qa = 768 + 4 * s4
                    for chunk in range(8):
                        g = chunk // 2
                        segs = [
                            (lambda pb, g=g, s4=s4: (KCT.ap[pb:pb + 64, s4 * 4 + g, 0:128], KCT.reg(s4 * 4 + g, s4 * 4 + g + 1, 0, 128)),
                             VCB.ap[:, s4, g * 64:(g + 1) * 64], VCB.reg(s4, s4 + 1, 0, 256), 128),
                            (lambda pb, g=g, qa=qa: (KD.ap[pb:pb + 64, g, qa:qa + 4], KD.reg(g, g + 1, qa, qa + 4)),
                             VS.ap[0:4, s4, g * 64:(g + 1) * 64], VS.reg(s4, s4 + 1, 0, 256), 4),
                        ]
                        ux = len(units)
                        units.append((attn_unit(l, 4, qa, chunk, g, segs, mask_t[0:4, 2, 0:132], 132, ux),
                                      (lambda qa=qa, ux=ux: attn_finish(4, qa, (ux + 1) % 2)) if chunk == 7 else None))
            run_units(units)
            if STAGE < 7:
                return
            for j in range(0, KC, 2):
                sga, rga = wtile(wv(w_in, l, 0, 16, OGA + j * 128, 256), 16, 256)
                swa, rwa = wtile(wv(wba, l, 0, 8, j * 128, 256), 8, 256)
                for (a, b) in tiles_r:
                    n = b - a
                    for hf in range(2):
                        bg, ba_ = psf(), psf()
                        pg, pgr = PSF.s(bg, 0, n)
                        pa, par = PSF.s(ba_, 0, n)
                        for k in range(KC):
                            x_ap, x_r = XB.s(k, a, b)
                            mm(pg, wr_t[:, sga, k, hf * 128:(hf + 1) * 128], x_ap, k == 0, k == KC - 1, [rga, x_r], [pgr])
                        for k in range(8):
                            t_ap, t_r = ATT.s(k, a, b)
                            mm(pa, wr_t[:, swa, k, hf * 128:(hf + 1) * 128], t_ap, k == 0, k == 7, [rwa, t_r], [par])
                        si = nxt("sg", 2)
                        s_ap, s_r = SG2.s(si, 0, n)
                        P.op("act", lambda e, o=s_ap, i=pg: e.activation(out=o, in_=i, func=AF.Sigmoid), reads=[pgr], writes=[s_r])
                        o_ap, o_r = PA.s(j + hf, a, b)
                        P.op("dve", lambda e, o=o_ap, i0=s_ap, i1=pa: e.tensor_tensor(out=o, in0=i0, in1=i1, op=ALU.mult), reads=[s_r, par], writes=[o_r])
            if STAGE < 8:
                return
            if samp:
                srcs = sc[l].rearrange("s t c -> (s t) c")
                for h2 in range(2):
                    ld("sp", osb_t[0:8, h2, :], OSB.reg(h2, h2 + 1, 0, 512), srcs[:, h2 * 512:(h2 + 1) * 512], ("k", h2))
            for i2 in range(4):
                scc, rcc = wtile(wv(w_in, l, 0, 16, OCC + i2 * 256, 256), 16, 256)
                sch, rch = wtile(wv(w_in, l, 0, 16, OCH + i2 * 256, 256), 16, 256)
                scb, rcb = wtile(wv(w_in, l, 0, 16, OCB + i2 * 256, 256), 16, 256)
                for hf in range(2):
                    i = 2 * i2 + hf
                    ui = nxt("uu", 2)
                    uall = UU.reg(ui, ui + 1, 0, 800)
                    usv = UU.ap[:, ui, 770:794].rearrange("p (s t) -> p s t", t=6)
                    if not samp:
                        P.op("dve", lambda e, ui=ui: e.memset(UU.ap[:, ui, c0:c0 + 2], 0.0), writes=[uall])
                    else:
                        P.op("dve", lambda e, ui=ui, i=i: e.tensor_copy(out=UU.ap[:, ui, 0:2], in_=ust_t[:, l * 8 + i, :]),
                             reads=[UST.reg(l * 8 + i, l * 8 + i + 1, 0, 2)], writes=[uall])
                        bo = psf()
                        P.op("pe", lambda e, i=i, bo=bo: e.transpose(psf_t[:, bo, 0:8], osb_t[0:8, i // 4, (i % 4) * 128:(i % 4 + 1) * 128], idf_t[0:8, 0, 0:8]),
                             reads=[OSB.reg(i // 4, i // 4 + 1, 0, 512), IDF.reg(0, 1, 0, 128)], writes=[PSF.reg(bo, bo + 1, 0, 512)])
                        P.op("dve", lambda e, usv=usv, bo=bo: e.tensor_copy(out=usv[:, :, 0:2], in_=psf_t[:, bo, 0:8].rearrange("p (s t) -> p s t", t=2)),
                             reads=[PSF.reg(bo, bo + 1, 0, 512)], writes=[uall])
                    for (a, b) in tiles:
                        n = b - a
                        b1, b2, b3 = psf(), psf(), psf()
                        pcc, pccr = PSF.s(b1, 0, n)
                        pch, pchr = PSF.s(b2, 0, n)
                        pcb, pcbr = PSF.s(b3, 0, n)
                        for (pp_, ppr_, sl_, rg_) in ((pcc, pccr, scc, rcc), (pch, pchr, sch, rch), (pcb, pcbr, scb, rcb)):
                            for k in range(KC):
                                x_ap, x_r = XB.s(k, a, b)
                                mm(pp_, wr_t[:, sl_, k, hf * 128:(hf + 1) * 128], x_ap, k == 0, k == KC - 1, [rg_, x_r], [ppr_])
                        cc_ap, cc_r = CCF.s(0, a, b)
                        cb_ap, cb_r = CBF.s(0, a, b)
                        P.op("act", lambda e, o=cc_ap, i=pcc: e.copy(out=o, in_=i), reads=[pccr], writes=[cc_r])
                        P.op("act", lambda e, o=cb_ap, i=pcb: e.copy(out=o, in_=i), reads=[pcbr], writes=[cb_r])
                        pe_ = min(b, 768)
                        if a < pe_:
                            P.op("dve", lambda e, ui=ui, a=a, pe_=pe_, b2=b2: e.tensor_tensor(
                                out=UU.ap[:, ui, 2 + a:2 + pe_], in0=CCF.ap[:, 0, a:pe_], in1=psf_t[:, b2, 0:pe_ - a], op=ALU.mult),
                                reads=[cc_r, pchr], writes=[uall])
                        if b > 768:
                            so = 768 - a
                            P.op("dve", lambda e, usv=usv, so=so, b2=b2: e.tensor_tensor(
                                out=usv[:, :, 2:6], in0=CCF.ap[:, 0, 768:784].rearrange("p (s t) -> p s t", t=4),
                                in1=psf_t[:, b2, so:so + 16].rearrange("p (s t) -> p s t", t=4), op=ALU.mult),
                                reads=[cc_r, pchr], writes=[uall])
                    if not samp:
                        P.op("dve", lambda e, ui=ui: e.tensor_scalar_mul(out=UU.ap[:, ui, 512:514], in0=UU.ap[:, ui, 512:514], scalar1=flag_t[:, 0, 0:1]),
                             reads=[uall, FLAG.reg(0, 1, 0, 1)], writes=[uall])
                    w0 = cw_t[:, 0, (l * 3 + 0) * 8 + i:(l * 3 + 0) * 8 + i + 1]
                    w1 = cw_t[:, 0, (l * 3 + 1) * 8 + i:(l * 3 + 1) * 8 + i + 1]
                    w2 = cw_t[:, 0, (l * 3 + 2) * 8 + i:(l * 3 + 2) * 8 + i + 1]
                    cwr = CW.reg(0, 1, 0, 96)
                    ccall = CCF.reg(0, 1, 0, WB)
                    cball = CBF.reg(0, 1, 0, WB)
                    acc = CCF.ap[:, 0, c0:768]
                    P.op("dve", lambda e, ui=ui, acc=acc, w0=w0: e.tensor_scalar_mul(out=acc, in0=UU.ap[:, ui, c0:768], scalar1=w0),
                         reads=[uall, cwr], writes=[ccall])
                    P.op("dve", lambda e, ui=ui, acc=acc, w1=w1: e.scalar_tensor_tensor(out=acc, in0=UU.ap[:, ui, c0 + 1:769], scalar=w1, in1=acc, op0=ALU.mult, op1=ALU.add),
                         reads=[uall, cwr, ccall], writes=[ccall])
                    P.op("dve", lambda e, ui=ui, acc=acc, w2=w2: e.scalar_tensor_tensor(out=acc, in0=UU.ap[:, ui, c0 + 2:770], scalar=w2, in1=acc, op0=ALU.mult, op1=ALU.add),
                         reads=[uall, cwr, ccall], writes=[ccall])
                    y_ap, y_r = YC.s(i, c0, 768)
                    P.op("dve", lambda e, o=y_ap, acc=acc: e.tensor_tensor(out=o, in0=acc, in1=CBF.ap[:, 0, c0:768], op=ALU.mult),
                         reads=[ccall, cball], writes=[y_r])
                    if samp:
                        accs = CCF.ap[:, 0, 768:784].rearrange("p (s t) -> p s t", t=4)
                        cbs = CBF.ap[:, 0, 768:784].rearrange("p (s t) -> p s t", t=4)
                        P.op("dve", lambda e, usv=usv, accs=accs, w0=w0: e.tensor_scalar_mul(out=accs, in0=usv[:, :, 0:4], scalar1=w0),
                             reads=[uall, cwr], writes=[ccall])
                        P.op("dve", lambda e, usv=usv, accs=accs, w1=w1: e.scalar_tensor_tensor(out=accs, in0=usv[:, :, 1:5], scalar=w1, in1=accs, op0=ALU.mult, op1=ALU.add),
                             reads=[uall, cwr, ccall], writes=[ccall])
                        P.op("dve", lambda e, usv=usv, accs=accs, w2=w2: e.scalar_tensor_tensor(out=accs, in0=usv[:, :, 2:6], scalar=w2, in1=accs, op0=ALU.mult, op1=ALU.add),
                             reads=[uall, cwr, ccall], writes=[ccall])
                        ys_ap, ys_r = YC.s(i, 768, 784)
                        P.op("dve", lambda e, o=ys_ap, accs=accs, cbs=cbs: e.tensor_tensor(out=o.rearrange("p (s t) -> p s t", t=4), in0=accs, in1=cbs, op=ALU.mult),
                             reads=[ccall, cball], writes=[ys_r])
                        P.op("dve", lambda e, ui=ui, i=i: e.tensor_copy(out=uo_t[:, i, 0:2], in_=UU.ap[:, ui, 768:770]), reads=[uall], writes=[UO.reg(i, i + 1, 0, 10)])
                        P.op("dve", lambda e, usv=usv, i=i: e.tensor_copy(out=uo_t[:, i, 2:10].rearrange("p (s t) -> p s t", t=2), in_=usv[:, :, 4:6]),
                             reads=[uall], writes=[UO.reg(i, i + 1, 0, 10)])
                    else:
                        P.op("dve", lambda e, ui=ui, i=i: e.tensor_copy(out=ust_t[:, l * 8 + i, :], in_=UU.ap[:, ui, 768:770]), reads=[uall],
                             writes=[UST.reg(l * 8 + i, l * 8 + i + 1, 0, 2)])
            if samp:
                for (c_lo, c_hi, nrow, dst) in ((0, 2, 2, convp[l]), (2, 10, 8, csmp[l])):
                    for h2 in range(2):
                        bo = psf()
                        for ii in range(4):
                            i = h2 * 4 + ii
                            P.op("pe", lambda e, i=i, ii=ii, bo=bo, c_lo=c_lo, c_hi=c_hi, nrow=nrow: e.transpose(
                                psf_t[0:nrow, bo, ii * 128:(ii + 1) * 128], uo_t[:, i, c_lo:c_hi], idf_t[:, 0, :]),
                                reads=[UO.reg(i, i + 1, 0, 10), IDF.reg(0, 1, 0, 128)], writes=[PSF.reg(bo, bo + 1, 0, 512)])
                        oi = nxt("o", 2)
                        P.op("act", lambda e, oi=oi, bo=bo, nrow=nrow: e.copy(out=osb_t[0:nrow, oi, :], in_=psf_t[0:nrow, bo, :]),
                             reads=[PSF.reg(bo, bo + 1, 0, 512)], writes=[OSB.reg(oi, oi + 1, 0, 512)])
                        store(dst[:, h2 * 512:(h2 + 1) * 512], osb_t[0:nrow, oi, :], OSB.reg(oi, oi + 1, 0, 512))
            if STAGE < 9:
                return
            for j in range(0, KC, 2):
                sgc, rgc = wtile(wv(w_in, l, 0, 16, OGC + j * 128, 256), 16, 256)
                swc, rwc = wtile(wv(wbc, l, 0, 8, j * 128, 256), 8, 256)
                for (a, b) in tiles_r:
                    n = b - a
                    for hf in range(2):
                        bg, bc_ = psf(), psf()
                        pg, pgr = PSF.s(bg, 0, n)
                        pc, pcr = PSF.s(bc_, 0, n)
                        for k in range(KC):
                            x_ap, x_r = XB.s(k, a, b)
                            mm(pg, wr_t[:, sgc, k, hf * 128:(hf + 1) * 128], x_ap, k == 0, k == KC - 1, [rgc, x_r], [pgr])
                        for k in range(8):
                            t_ap, t_r = YC.s(k, a, b)
                            mm(pc, wr_t[:, swc, k, hf * 128:(hf + 1) * 128], t_ap, k == 0, k == 7, [rwc, t_r], [pcr])
                        si = nxt("sg", 2)
                        s_ap, s_r = SG2.s(si, 0, n)
                        t_ap2, t_r2 = TM2.s(si, 0, n)
                        P.op("act", lambda e, o=s_ap, i=pg: e.activation(out=o, in_=i, func=AF.Sigmoid), reads=[pgr], writes=[s_r])
                        P.op("dve", lambda e, o=t_ap2, i0=s_ap, i1=pc: e.tensor_tensor(out=o, in0=i0, in1=i1, op=ALU.mult), reads=[s_r, pcr], writes=[t_r2])
                        o_ap, o_r = PA.s(j + hf, a, b)
                        P.op("dve", lambda e, o=o_ap, i0=t_ap2: e.tensor_tensor(out=o, in0=i0, in1=o, op=ALU.add), reads=[t_r2, o_r], writes=[o_r])
            for j in range(0, KC, 2):
                so, ro = wtile(wv(wout, l, 0, 16, j * 128, 256), 16, 256)
                for (a, b) in tiles_r:
                    n = b - a
                    for hf in range(2):
                        bo = psf()
                        po, por = PSF.s(bo, 0, n)
                        for k in range(KC):
                            m_ap, m_r = PA.s(k, a, b)
                            mm(po, wr_t[:, so, k, hf * 128:(hf + 1) * 128], m_ap, k == 0, k == KC - 1, [ro, m_r], [por])
                        r_ap, r_r = R.s(j + hf, a, b)
                        P.op("dve", lambda e, o=r_ap, i=po: e.tensor_tensor(out=o, in0=i, in1=o, op=ALU.add), reads=[por, r_r], writes=[r_r])

        st_keys = {"n": 0}

        def store(dst_ap, src_ap, src_reg):
            k = st_keys["n"] % 4
            st_keys["n"] += 1
            P.op("sp", lambda e: e.dma_start(out=dst_ap, in_=src_ap), reads=[src_reg], dma=("s", k))

        for ps_ in range(2):
            samp = (ps_ == 1)
            ld("sp", R_t[:], R.reg(0, KC, 0, WB), xin[ps_], ("x", 0))
            ld("sp", cos_t[:, 0, :], COS.reg(0, 1, 0, WB), cst_cos[ps_], ("x", 1))
            ld("sp", sin_t[:, 0, :], SIN.reg(0, 1, 0, WB), cst_sin[ps_], ("x", 2))
            for k in range(KC):
                r_ap, r_r = R.s(k, 0, WB)
                x_ap, x_r = XB.s(k, 0, WB)
                P.op("act", lambda e, o=x_ap, i=r_ap: e.copy(out=o, in_=i), reads=[r_r], writes=[x_r])
                P.op("dve", lambda e, o=r_ap: e.tensor_scalar_mul(out=o, in0=o, scalar1=ALPHA), reads=[r_r], writes=[r_r])
            for l in range(DEPTH):
                c0 = (512 - 128 * (DEPTH - l)) if ps_ == 0 else 0
                tiles = col_tiles(c0, 768, samp)
                tiles_r = col_tiles(c0 + 128, 768, samp) if ps_ == 0 else tiles
                if STAGE >= 1:
                    ffn(l, f1gu, f1d, tiles)
                if STAGE >= 2:
                    layernorm(l, 0, tiles)
                if STAGE >= 3:
                    MIX(l, ps_, c0, tiles, tiles_r)
                if STAGE >= 10:
                    layernorm(l, 1, tiles_r)
                    ffn(l, f2gu, f2d, tiles_r)
                    layernorm(l, 2, tiles_r, final=(l == DEPTH - 1))
            blocks = list(range(4, 6)) if ps_ == 0 else list(range(0, 6))
            for nb_ in blocks:
                ca = nb_ * 128
                row0 = (nb_ - 4) * 128 if ps_ == 0 else 256 + nb_ * 128
                for q4 in range(4):
                    bo = psf()
                    for kk in range(4):
                        k = q4 * 4 + kk
                        r_ap, r_r = R.s(k, ca, ca + 128)
                        o_ap, o_r = PSF.s(bo, kk * 128, kk * 128 + 128)
                        P.op("pe", lambda e, o=o_ap, i=r_ap: e.transpose(o, i, idf_t[:, 0, :]),
                             reads=[r_r, IDF.reg(0, 1, 0, 128)], writes=[o_r])
                    oi = nxt("o", 2)
                    s_ap, s_r = OSB.s(oi, 0, 512)
                    p_ap, p_r = PSF.s(bo, 0, 512)
                    P.op("act", lambda e, o=s_ap, i=p_ap: e.copy(out=o, in_=i), reads=[p_r], writes=[s_r])
                    store(y_own[row0:row0 + 128, q4 * 512:(q4 + 1) * 512], s_ap, s_r)
            if samp:
                for q4 in range(4):
                    bo = psf()
                    for kk in range(4):
                        k = q4 * 4 + kk
                        r_ap, r_r = R.s(k, 768, 784)
                        o_ap = psf_t[0:16, bo, kk * 128:(kk + 1) * 128]
                        o_r = PSF.reg(bo, bo + 1, 0, 512)
                        P.op("pe", lambda e, o=o_ap, i=r_ap: e.transpose(o, i, idf_t[:, 0, :]),
                             reads=[r_r, IDF.reg(0, 1, 0, 128)], writes=[o_r])
                    oi = nxt("o", 2)
                    s_ap = osb_t[0:16, oi, :]
                    s_r = OSB.reg(oi, oi + 1, 0, 512)
                    p_ap = psf_t[0:16, bo, :]
                    P.op("act", lambda e, o=s_ap, i=p_ap: e.copy(out=o, in_=i), reads=[PSF.reg(bo, bo + 1, 0, 512)], writes=[s_r])
                    store(y_smp[:, q4 * 512:(q4 + 1) * 512], s_ap, s_r)

        P.analyze()
        dkeys = sorted(P.dma_final.keys(), key=str)
        esem = {e: es.enter_context(nc.semaphore("e_" + e)) for e in ("pe", "act", "dve", "pool", "sp")}
        dsem = {k: es.enter_context(nc.semaphore("d_%s_%s" % (k[0], k[1]))) for k in dkeys}
        for v in list(P.eng_final.values()) + list(P.dma_final.values()):
            assert v < 60000, v
        block = es.enter_context(nc.Block())
        per_eng = {}
        for o in P.ops:
            per_eng.setdefault(o.eng, []).append(o)

        def run_engine(eng_name, e):
            waited = {}
            for o in per_eng.get(eng_name, []):
                for d in o.deps:
                    if d.dma is not None:
                        key, val, sem = ("d", d.dma), d.dmaval, dsem[d.dma]
                    else:
                        key, val, sem = ("e", d.eng), d.sigval, esem[d.eng]
                    if waited.get(key, 0) >= val:
                        continue
                    waited[key] = val
                    e.wait_ge(sem, val)
                ins = o.fn(e)
                if o.dma is not None:
                    ins.then_inc(dsem[o.dma], 16)
                elif o.sig:
                    ins.then_inc(esem[o.eng], 1)
            if eng_name == "sp":
                for k in dkeys:
                    if k[0] == "s":
                        e.wait_ge(dsem[k], P.dma_final[k])

        @block.tensor
        def _(e):
            run_engine("pe", e)

        @block.scalar
        def _(e):
            run_engine("act", e)

        @block.vector
        def _(e):
            run_engine("dve", e)

        @block.gpsimd
        def _(e):
            run_engine("pool", e)

        @block.sync
        def _(e):
            run_engine("sp", e)
    return nc


_NC_CACHE = {}


def _consts(half):
    inv_freq = (10000.0 ** (-np.arange(0, 64, 2, dtype=np.float32) / 64)).astype(np.float32)
    cos_t = np.zeros((2, 128, WB), np.float32)
    sin_t = np.zeros((2, 128, WB), np.float32)
    p = np.arange(128)
    d = p % 64
    f = d % 32
    sign = np.where(d < 32, -1.0, 1.0).astype(np.float32)
    for ps_ in range(2):
        pos = np.zeros(WB, np.int32)
        pos[:768] = half * 1024 - 512 + ps_ * 768 + np.arange(768)
        if ps_ == 1:
            pos[768:] = 16384 + (np.arange(16) % 4)
        ang = pos.astype(np.float32)[:, None] * inv_freq[None, :]
        c = np.cos(ang).astype(np.float32)
        s = np.sin(ang).astype(np.float32)
        cos_t[ps_] = c[:, f].T
        sin_t[ps_] = s[:, f].T * sign[:, None]
    i = np.arange(128)[:, None]
    j = np.arange(128)[None, :]
    mask = np.full((128, 4, 256), NEG, np.float32)
    prev = np.where(j >= i, 0.0, NEG)
    cur = np.where(j <= i, 0.0, NEG)
    mask[:, 0, :128] = prev
    mask[:, 0, 128:] = cur
    mask[:, 1, :128] = prev if half == 1 else NEG
    mask[:, 1, 128:] = cur
    mask[:, 2, :128] = prev
    mask[:, 2, 128:132] = np.where(np.arange(4)[None, :] <= i, 0.0, NEG)
    ident = np.eye(128, dtype=np.float32)
    perm = np.zeros((128, 128), np.float32)
    for m in range(128):
        k = m + 32 if (m % 64) < 32 else m - 32
        perm[k, m] = 1.0
    flag = np.full((128, 1), 1.0 if half == 1 else 0.0, np.float32)
    return cos_t, sin_t, mask, ident, perm, flag


def kernel(x_prompt, x_sample, cache_k_win, cache_v_win, state_conv, ln_g, ln_b, w_in, sinks,
           conv_w, w_branch_attn, w_branch_conv, w_out, ffn1_gu, ffn1_down, ffn2_gu, ffn2_down):
    f32 = np.float32
    A = lambda a: np.ascontiguousarray(np.asarray(a, dtype=f32))
    x_prompt, x_sample = A(x_prompt), A(x_sample)
    cache_k_win, cache_v_win, state_conv = A(cache_k_win), A(cache_v_win), A(state_conv)
    shared = {
        "lng": A(A(ln_g).reshape(DEPTH, 3, 16, 128).transpose(3, 0, 1, 2).reshape(128, DEPTH * 48)),
        "lnb": A(A(ln_b).reshape(DEPTH, 3, 16, 128).transpose(3, 0, 1, 2).reshape(128, DEPTH * 48)),
        "w_in": A(w_in),
        "sinks": A(np.broadcast_to(A(sinks).reshape(1, DEPTH * 16), (128, DEPTH * 16))),
        "convw": A(A(conv_w).reshape(DEPTH, 3, 8, 128).transpose(3, 0, 1, 2).reshape(128, DEPTH * 24)),
        "wba": A(w_branch_attn), "wbc": A(w_branch_conv), "wout": A(w_out),
        "f1gu": A(ffn1_gu), "f1d": A(ffn1_down), "f2gu": A(ffn2_gu), "f2d": A(ffn2_down),
    }
    in_maps = []
    for c in range(8):
        b, half = c // 2, c % 2
        ext = np.zeros((1536, DM), f32)
        if half == 1:
            ext[:] = x_prompt[b, 512:2048]
        else:
            ext[512:] = x_prompt[b, 0:1024]
        xin = np.zeros((2, 128, KC, WB), f32)
        for ps_ in range(2):
            cols = np.zeros((WB, DM), f32)
            cols[:768] = ext[ps_ * 768:(ps_ + 1) * 768]
            if ps_ == 1:
                cols[768:] = x_sample[4 * c:4 * c + 4].reshape(16, DM)
            xin[ps_] = cols.T.reshape(KC, 128, WB).transpose(1, 0, 2)
        cos_t, sin_t, mask, ident, perm, flag = _consts(half)
        m = dict(shared)
        m.update({
            "xin": xin,
            "ck": A(cache_k_win[:, 4 * c:4 * c + 4].reshape(DEPTH, 4, 128, 256)),
            "cv": A(cache_v_win[:, 4 * c:4 * c + 4].reshape(DEPTH, 4, 128, 256)),
            "sc": A(state_conv[:, 4 * c:4 * c + 4]),
            "cst_cos": cos_t, "cst_sin": sin_t, "cst_mask": mask, "cst_ident": ident,
            "cst_perm": perm, "cst_flag": flag,
        })
        in_maps.append(m)
    if "nc" not in _NC_CACHE:
        _NC_CACHE["nc"] = build_program()
    nc = _NC_CACHE["nc"]
    ncr = int(os.environ.get("MK_CORES", "8"))
    res = run_bass_kernel_spmd(nc, in_maps[:ncr], core_ids=list(range(ncr)))
    R_ = res.results
    y_prompt = np.zeros((4, 2048, DM), f32)
    y_sample = np.zeros((32, 4, DM), f32)
    kwp = np.zeros((DEPTH, 4, 128, 4, 64), f32)
    vwp = np.zeros((DEPTH, 4, 128, 4, 64), f32)
    cvp = np.zeros((DEPTH, 4, 2, 1024), f32)
    kws = np.zeros((DEPTH, 32, 128, 4, 64), f32)
    vws = np.zeros((DEPTH, 32, 128, 4, 64), f32)
    cvs = np.zeros((DEPTH, 32, 2, 1024), f32)
    for c in range(ncr):
        b, half = c // 2, c % 2
        r = R_[c]
        y_prompt[b, half * 1024:(half + 1) * 1024] = np.asarray(r["y_own"])
        y_sample[4 * c:4 * c + 4] = np.asarray(r["y_smp"]).reshape(4, 4, DM)
        if half == 1:
            kwp[:, b] = np.asarray(r["kwin"]).reshape(DEPTH, 128, 4, 64)
            vwp[:, b] = np.asarray(r["vwin"]).reshape(DEPTH, 128, 4, 64)
            cvp[:, b] = np.asarray(r["convp"])
        kws[:, 4 * c:4 * c + 4] = np.asarray(r["ksmp"]).reshape(DEPTH, 4, 128, 4, 64)
        vws[:, 4 * c:4 * c + 4] = np.asarray(r["vsmp"]).reshape(DEPTH, 4, 128, 4, 64)
        cvs[:, 4 * c:4 * c + 4] = np.asarray(r["csmp"]).reshape(DEPTH, 4, 2, 1024)
    return (y_prompt, y_sample, kwp, vwp, cvp, kws, vws, cvs)
```
